# Optimizing a Trainium2 kernel written in Bass

```python
import jax, jax.numpy as jnp
from jax import lax
import numpy as np

D_MODEL = 1024
BATCH = 32
SEQ = 2048
DEPTH = 4
DEC_BATCH = 32
DEC_SEQ = 32
PAST_LEN = 2048

CHUNK = 64
N_META = 16
WINDOW = 128
WINDOW_CHUNKS = WINDOW // CHUNK
SWA_CACHE = min(WINDOW, PAST_LEN)
ATTN_HEADS = 8
ATTN_KV_HEADS = 2
HEAD_DIM = 64
GROUP = ATTN_HEADS // ATTN_KV_HEADS
ATTN_SCALE = HEAD_DIM ** -0.5
RET_HEADS = 8
RET_DK = 64
RET_DV = 64
ATTN_WIDTH = ATTN_HEADS * HEAD_DIM
KV_WIDTH = ATTN_KV_HEADS * HEAD_DIM
RET_QK_WIDTH = RET_HEADS * RET_DK
RET_V_WIDTH = RET_HEADS * RET_DV
MIX_WIDTH = ATTN_WIDTH + RET_V_WIDTH
IN_WIDTH = ATTN_WIDTH + 2 * KV_WIDTH + 2 * RET_QK_WIDTH + 2 * RET_V_WIDTH
SPLIT_POINTS = (ATTN_WIDTH,
                ATTN_WIDTH + KV_WIDTH,
                ATTN_WIDTH + 2 * KV_WIDTH,
                ATTN_WIDTH + 2 * KV_WIDTH + RET_QK_WIDTH,
                ATTN_WIDTH + 2 * KV_WIDTH + 2 * RET_QK_WIDTH,
                ATTN_WIDTH + 2 * KV_WIDTH + 2 * RET_QK_WIDTH + RET_V_WIDTH)
D_FF = 2816
ROPE_BASE = 10000.0
EPS = 1e-6
NEG = -1e30

kernel_name = 'hymba_swa_retention_macaron_stream_step'


def rmsnorm(x, g):
    x32 = x.astype(jnp.float32)
    y = x32 * lax.rsqrt(jnp.mean(x32 * x32, axis=-1, keepdims=True) + EPS)
    return y.astype(x.dtype) * g


def half_ffn(x, g_pre, w_in, w_out, g_post):
    gate, up = jnp.split(rmsnorm(x, g_pre) @ w_in, 2, axis=-1)
    return x + 0.5 * rmsnorm((jax.nn.silu(gate) * up) @ w_out, g_post)


def project(h, w_in):
    return jnp.split(h @ w_in, SPLIT_POINTS, axis=-1)


def sink_softmax(scores, sink):
    m = jnp.maximum(jnp.max(scores, axis=-1, keepdims=True), sink)
    p = jnp.exp(scores - m)
    return p / (jnp.sum(p, axis=-1, keepdims=True) + jnp.exp(sink - m))


def rotate_half(x, cos, sin, axis):
    x1, x2 = jnp.split(x, 2, axis=axis)
    return jnp.concatenate([x1 * cos - x2 * sin, x2 * cos + x1 * sin], axis=axis)


def swa_prompt(q, k, v, key_valid, sinks):
    B, Lp, _ = q.shape
    nC = Lp // CHUNK
    qb = q.reshape(B, nC, CHUNK, ATTN_KV_HEADS, GROUP, HEAD_DIM)
    k = k.reshape(B, Lp, ATTN_KV_HEADS, HEAD_DIM)
    v = v.reshape(B, Lp, ATTN_KV_HEADS, HEAD_DIM)
    lead = WINDOW_CHUNKS * CHUNK
    kp = jnp.pad(k, ((0, 0), (lead, 0), (0, 0), (0, 0)))
    vp = jnp.pad(v, ((0, 0), (lead, 0), (0, 0), (0, 0)))
    okp = jnp.pad(key_valid, (lead, 0))

    def band(a):
        return jnp.concatenate(
            [a[:, j * CHUNK: j * CHUNK + Lp].reshape((B, nC, CHUNK) + a.shape[2:])
             for j in range(WINDOW_CHUNKS + 1)], axis=2)

    kb, vb = band(kp), band(vp)
    okb = jnp.concatenate([okp[j * CHUNK: j * CHUNK + Lp].reshape(nC, CHUNK)
                           for j in range(WINDOW_CHUNKS + 1)], axis=1)
    s = jnp.einsum('bnqkgd,bnskd->bnkgqs', qb, kb).astype(jnp.float32) * ATTN_SCALE
    s = jnp.where(okb[None, :, None, None, None, :], s, NEG)
    sink = sinks.astype(jnp.float32).reshape(ATTN_KV_HEADS, GROUP)[None, None, :, :, None, None]
    p = sink_softmax(s, sink)
    o = jnp.einsum('bnkgqs,bnskd->bnqkgd', p.astype(v.dtype), vb)
    return o.reshape(B, Lp, ATTN_WIDTH)


def swa_sample(q, k, v, cache_k, cache_v, sinks):
    B, T, _ = q.shape
    qg = q.reshape(B, T, ATTN_KV_HEADS, GROUP, HEAD_DIM)
    kc = jnp.concatenate([cache_k, k.reshape(B, T, ATTN_KV_HEADS, HEAD_DIM)], axis=1)
    vc = jnp.concatenate([cache_v, v.reshape(B, T, ATTN_KV_HEADS, HEAD_DIM)], axis=1)
    s = jnp.einsum('btkgd,bskd->bkgts', qg, kc).astype(jnp.float32) * ATTN_SCALE
    sink = sinks.astype(jnp.float32).reshape(ATTN_KV_HEADS, GROUP)[None, :, :, None, None]
    p = sink_softmax(s, sink)
    o = jnp.einsum('bkgts,bskd->btkgd', p.astype(vc.dtype), vc)
    return o.reshape(B, T, ATTN_WIDTH), kc[:, -SWA_CACHE:], vc[:, -SWA_CACHE:]


def retention_blocks(q, k, v, s0):
    q, k, v = q.astype(jnp.float32), k.astype(jnp.float32), v.astype(jnp.float32)
    T = q.shape[2]
    log_g = jnp.log1p(-jnp.exp2(-5.0 - jnp.arange(RET_HEADS, dtype=jnp.float32)))
    freqs = ROPE_BASE ** (-jnp.arange(RET_DK // 2, dtype=jnp.float32) / (RET_DK // 2))
    pos = jnp.arange(T, dtype=jnp.float32)
    ang = pos[:, None] * freqs[None, :]
    cos, sin = jnp.cos(ang)[:, None, :], jnp.sin(ang)[:, None, :]
    qr = rotate_half(q, cos, sin, -1) * RET_DK ** -0.5
    kr = rotate_half(k, cos, sin, -1)
    diff = pos[:, None] - pos[None, :]
    intra_decay = jnp.where(diff >= 0, jnp.exp(log_g[:, None, None] * jnp.maximum(diff, 0.0)), 0.0)
    scores = jnp.einsum('bnihd,bnjhd->bnhij', qr, kr) * intra_decay
    intra = jnp.einsum('bnhij,bnjhe->bnihe', scores, v)
    kv_decay = jnp.exp((T - pos)[:, None] * log_g[None, :])
    w_blk = jnp.einsum('bnihd,bnihe->bnhde', kr * kv_decay[:, :, None], v)
    g_T = jnp.exp(T * log_g)[:, None, None]
    shift = -T * freqs
    cos_s, sin_s = jnp.cos(shift)[:, None], jnp.sin(shift)[:, None]

    def step(s, w):
        return rotate_half(g_T * s + w, cos_s, sin_s, -2), s

    s_final, s_before = lax.scan(step, s0.astype(jnp.float32), jnp.moveaxis(w_blk, 1, 0))
    s_before = jnp.moveaxis(s_before, 0, 1)
    q_decay = jnp.exp(pos[:, None] * log_g[None, :])[:, :, None]
    cross = jnp.einsum('bnihd,bnhde->bnihe', qr, s_before) * q_decay
    return intra + cross, s_final


def retention_out(o, gate):
    o = o * lax.rsqrt(jnp.mean(o * o, axis=-1, keepdims=True) + EPS)
    return o.reshape(o.shape[:2] + (RET_V_WIDTH,)).astype(gate.dtype) * jax.nn.silu(gate)


def mixer_prompt(h, w_in, sinks, key_valid):
    B, Lp, _ = h.shape
    nC = Lp // CHUNK
    qa, ka, va, qr, kr, vr, gr = project(h, w_in)
    attn = swa_prompt(qa, ka, va, key_valid, sinks)
    kr = kr * key_valid[None, :, None].astype(kr.dtype)
    s0 = jnp.zeros((B, RET_HEADS, RET_DK, RET_DV), jnp.float32)
    o, s_final = retention_blocks(qr.reshape(B, nC, CHUNK, RET_HEADS, RET_DK),
                                  kr.reshape(B, nC, CHUNK, RET_HEADS, RET_DK),
                                  vr.reshape(B, nC, CHUNK, RET_HEADS, RET_DV), s0)
    ret = retention_out(o.reshape(B, Lp, RET_HEADS, RET_DV), gr)
    new_k = ka[:, -SWA_CACHE:].reshape(B, SWA_CACHE, ATTN_KV_HEADS, HEAD_DIM)
    new_v = va[:, -SWA_CACHE:].reshape(B, SWA_CACHE, ATTN_KV_HEADS, HEAD_DIM)
    return jnp.concatenate([attn, ret], axis=-1), new_k, new_v, s_final


def mixer_sample(h, w_in, sinks, cache_k, cache_v, state):
    B, T, _ = h.shape
    qa, ka, va, qr, kr, vr, gr = project(h, w_in)
    attn, new_k, new_v = swa_sample(qa, ka, va, cache_k, cache_v, sinks)
    o, s_new = retention_blocks(qr.reshape(B, 1, T, RET_HEADS, RET_DK),
                                kr.reshape(B, 1, T, RET_HEADS, RET_DK),
                                vr.reshape(B, 1, T, RET_HEADS, RET_DV), state)
    ret = retention_out(o.reshape(B, T, RET_HEADS, RET_DV), gr)
    return jnp.concatenate([attn, ret], axis=-1), new_k, new_v, s_new


def setup_inputs(seed: int = 0) -> dict:
    key = jax.random.key(seed)
    ks = jax.random.split(key, 20)

    def nrm(k, shape, scale):
        return scale * jax.random.normal(k, shape, jnp.float32)

    def gain(k, shape):
        return 1.0 + 0.1 * jax.random.normal(k, shape, jnp.float32)

    return {
        'x_prompt': nrm(ks[0], (BATCH, SEQ, D_MODEL), 1.0),
        'x_sample': nrm(ks[1], (DEC_BATCH, DEC_SEQ, D_MODEL), 1.0),
        'cache_swa_k': nrm(ks[2], (DEPTH, DEC_BATCH, SWA_CACHE, ATTN_KV_HEADS, HEAD_DIM), 1.0),
        'cache_swa_v': nrm(ks[3], (DEPTH, DEC_BATCH, SWA_CACHE, ATTN_KV_HEADS, HEAD_DIM), 1.0),
        'state_ret': nrm(ks[4], (DEPTH, DEC_BATCH, RET_HEADS, RET_DK, RET_DV), 1.0),
        'meta_tokens': nrm(ks[5], (N_META, D_MODEL), 1.0),
        'w_in': nrm(ks[6], (DEPTH, D_MODEL, IN_WIDTH), D_MODEL ** -0.5),
        'w_out': nrm(ks[7], (DEPTH, MIX_WIDTH, D_MODEL), MIX_WIDTH ** -0.5),
        'attn_sinks': nrm(ks[8], (DEPTH, ATTN_HEADS), 0.5),
        'ffn1_w_in': nrm(ks[9], (DEPTH, D_MODEL, 2 * D_FF), D_MODEL ** -0.5),
        'ffn1_w_out': nrm(ks[10], (DEPTH, D_FF, D_MODEL), D_FF ** -0.5),
        'ffn2_w_in': nrm(ks[11], (DEPTH, D_MODEL, 2 * D_FF), D_MODEL ** -0.5),
        'ffn2_w_out': nrm(ks[12], (DEPTH, D_FF, D_MODEL), D_FF ** -0.5),
        'norm_ffn1_pre': gain(ks[13], (DEPTH, D_MODEL)),
        'norm_ffn1_post': gain(ks[14], (DEPTH, D_MODEL)),
        'norm_mix_pre': gain(ks[15], (DEPTH, D_MODEL)),
        'norm_mix_post': gain(ks[16], (DEPTH, D_MODEL)),
        'norm_ffn2_pre': gain(ks[17], (DEPTH, D_MODEL)),
        'norm_ffn2_post': gain(ks[18], (DEPTH, D_MODEL)),
        'final_norm': gain(ks[19], (D_MODEL,)),
    }


def reference(x_prompt, x_sample, cache_swa_k, cache_swa_v, state_ret, meta_tokens, w_in, w_out,
              attn_sinks, ffn1_w_in, ffn1_w_out, ffn2_w_in, ffn2_w_out, norm_ffn1_pre, norm_ffn1_post,
              norm_mix_pre, norm_mix_post, norm_ffn2_pre, norm_ffn2_post, final_norm):
    B, S, D = x_prompt.shape
    lead = CHUNK - N_META
    Lp = S + CHUNK
    xp = jnp.concatenate([jnp.zeros((B, lead, D), x_prompt.dtype),
                          jnp.broadcast_to(meta_tokens.astype(x_prompt.dtype), (B, N_META, D)),
                          x_prompt], axis=1)
    key_valid = jnp.arange(Lp) >= lead
    xs = x_sample
    pk, pv, ps, sk, sv, ss = [], [], [], [], [], []
    for l in range(DEPTH):
        xp = half_ffn(xp, norm_ffn1_pre[l], ffn1_w_in[l], ffn1_w_out[l], norm_ffn1_post[l])
        xs = half_ffn(xs, norm_ffn1_pre[l], ffn1_w_in[l], ffn1_w_out[l], norm_ffn1_post[l])

        mp, k_p, v_p, s_p = mixer_prompt(rmsnorm(xp, norm_mix_pre[l]), w_in[l], attn_sinks[l], key_valid)
        xp = xp + rmsnorm(mp @ w_out[l], norm_mix_post[l])
        ms, k_s, v_s, s_s = mixer_sample(rmsnorm(xs, norm_mix_pre[l]), w_in[l], attn_sinks[l],
                                         cache_swa_k[l], cache_swa_v[l], state_ret[l])
        xs = xs + rmsnorm(ms @ w_out[l], norm_mix_post[l])
        pk.append(k_p); pv.append(v_p); ps.append(s_p)
        sk.append(k_s); sv.append(v_s); ss.append(s_s)

        xp = half_ffn(xp, norm_ffn2_pre[l], ffn2_w_in[l], ffn2_w_out[l], norm_ffn2_post[l])
        xs = half_ffn(xs, norm_ffn2_pre[l], ffn2_w_in[l], ffn2_w_out[l], norm_ffn2_post[l])

    y_prompt = rmsnorm(xp[:, CHUNK:], final_norm)
    y_sample = rmsnorm(xs, final_norm)
    prompt_swa_k = jnp.stack(pk)
    prompt_swa_v = jnp.stack(pv)
    prompt_ret = jnp.stack(ps)
    sample_swa_k = jnp.stack(sk)
    sample_swa_v = jnp.stack(sv)
    sample_ret = jnp.stack(ss)
    return (y_prompt, y_sample, prompt_swa_k, prompt_swa_v, prompt_ret, sample_swa_k, sample_swa_v, sample_ret)
```

```python
import bisect
from itertools import zip_longest
import numpy as np
import concourse.bass as bass
import concourse.mybir as mybir
from concourse.bass_utils import run_bass_kernel_spmd

F32 = mybir.dt.float32
BF16 = mybir.dt.bfloat16
AF = mybir.ActivationFunctionType
ALU = mybir.AluOpType
AX = mybir.AxisListType

D = 1024
KC = 8
DFF = 2816
NJ = 22
EPS = 1e-6
NEXT_F = 25
NTOK = 1280
NEG = -30000.0


class Buf:
    __slots__ = ("name", "w", "r", "excl")

    def __init__(self, name, excl=False):
        self.name = name
        self.w = None
        self.r = {}
        self.excl = excl


class TR:
    CE = ("pe", "act", "dve", "pool")
    QE = ("sp", "pool")

    def __init__(self, nslots=8):
        self.ops = {e: [] for e in ("pe", "act", "dve", "pool", "sp")}
        self.serial = {e: 0 for e in self.CE}
        self.cnt = {e: 0 for e in self.CE}
        self.sig = {e: [] for e in self.CE}
        self.sigser = {e: [] for e in self.CE}
        self.seen = {e: {} for e in self.ops}
        self.nslots = nslots
        self.pe_last = {}
        self.dcount = {}
        self.dnext = {q: 0 for q in self.QE}
        for q in self.QE:
            for k in range(nslots):
                self.dcount["d:%s%d" % (q, k)] = 0

    def sem_names(self):
        return list(self.CE) + list(self.dcount.keys())

    def _signal_upto(self, e, s):
        i = bisect.bisect_left(self.sigser[e], s)
        if i < len(self.sigser[e]):
            return self.sig[e][i]
        last = None
        for o in reversed(self.ops[e]):
            if o["kind"] == "c":
                last = o
                break
        assert last is not None and last["sig"] is None
        self.cnt[e] += 1
        last["sig"] = (e, 1)
        self.sigser[e].append(self.serial[e])
        self.sig[e].append(self.cnt[e])
        return self.cnt[e]

    def _wait(self, me, e, s):
        if e.startswith("d:"):
            c = s
        else:
            if e == me and e == "pe":
                return
            c = self._signal_upto(e, s)
        if self.seen[me].get(e, 0) >= c:
            return
        self.seen[me][e] = c
        self.ops[me].append({"kind": "w", "sem": e, "val": c})

    def _deps(self, me, reads, writes):
        deps = set()
        for b in reads:
            if b.w is not None:
                deps.add(b.w)
            if b.excl:
                for e, s in b.r.items():
                    if e != me:
                        deps.add((e, s))
        for b in writes:
            if b.w is not None:
                deps.add(b.w)
            for e, s in b.r.items():
                deps.add((e, s))
        for e, s in sorted(deps):
            self._wait(me, e, s)

    def op(self, eng, fn, reads=(), writes=(), pe_bank=None, pe_rows=None):
        if eng == "pe" and pe_bank is not None:
            b0, sz = pe_rows if pe_rows is not None else (0, 128)
            groups = set(range(b0 // 32, (b0 + sz - 1) // 32 + 1))
            last = self.pe_last.get(pe_bank)
            if last is not None and self.serial["pe"] - last[0] < 24 and not (groups & last[1]):
                c = self._signal_upto("pe", last[0])
                if self.seen["pe"].get("pe", 0) < c:
                    self.seen["pe"]["pe"] = c
                    self.ops["pe"].append({"kind": "w", "sem": "pe", "val": c})
            self.pe_last[pe_bank] = (self.serial["pe"] + 1, groups)
        self._deps(eng, reads, writes)
        self.serial[eng] += 1
        ser = self.serial[eng]
        self.ops[eng].append({"kind": "c", "fn": fn, "sig": None})
        for b in reads:
            b.r[eng] = ser
        for b in writes:
            b.w = (eng, ser)
            b.r = {}

    def dma(self, q, fn, reads=(), writes=()):
        k = self.dnext[q]
        self.dnext[q] = (k + 1) % self.nslots
        name = "d:%s%d" % (q, k)
        if self.dcount[name] > 0:
            self._wait(q, name, self.dcount[name])
        self._deps(q, reads, writes)
        self.dcount[name] += 16
        c = self.dcount[name]
        if q in self.serial:
            pass
        self.ops[q].append({"kind": "d", "fn": fn, "sig": (name, 16)})
        for b in reads:
            b.r[name] = c
        for b in writes:
            b.w = (name, c)
            b.r = {}

    def barrier(self):
        engs = ("pe", "act", "dve", "pool", "sp")
        for me in engs:
            for e in self.CE:
                if self.serial[e] > 0 and not (e == me):
                    self._wait_last(me, e)
            for name, c in self.dcount.items():
                if c > 0:
                    self._wait(me, name, c)

    def _wait_last(self, me, e):
        c = self._signal_upto(e, self.serial[e])
        if self.seen[me].get(e, 0) >= c:
            return
        self.seen[me][e] = c
        self.ops[me].append({"kind": "w", "sem": e, "val": c})

    def final_wait(self):
        for name, c in self.dcount.items():
            if c > 0:
                self._wait("sp", name, c)
        for e in self.CE:
            if self.serial[e] > 0:
                self._wait_last("sp", e)

    def replay(self, eng_name, eng, sems):
        for o in self.ops[eng_name]:
            if o["kind"] == "w":
                eng.wait_ge(sems[o["sem"]], o["val"])
            else:
                ins = o["fn"](eng)
                if o["sig"] is not None:
                    ins.then_inc(sems[o["sig"][0]], o["sig"][1])


def _tables(Tc, pad_first=0):
    H, DK = 8, 64
    g = 1.0 - np.exp2(-5.0 - np.arange(H, dtype=np.float64))
    logg = np.log(g)
    freqs = 10000.0 ** (-np.arange(32, dtype=np.float64) / 32.0)
    pos = np.arange(Tc, dtype=np.float64)
    ang = pos[:, None] * freqs[None, :]
    cos, sin = np.cos(ang), np.sin(ang)
    CQ = np.zeros((128, 4, 64)); SQ = np.zeros((128, 4, 64)); CK = np.zeros((128, 4, 64)); SK = np.zeros((128, 4, 64))
    for par in range(2):
        for d in range(64):
            f = d % 32
            sgn = -1.0 if d < 32 else 1.0
            for m in range(4):
                h = 2 * m + par
                dq = np.exp(logg[h] * pos) * DK ** -0.5
                dk = np.exp(-logg[h] * pos)
                CQ[par * 64 + d, m, :Tc] = cos[:, f] * dq
                SQ[par * 64 + d, m, :Tc] = sgn * sin[:, f] * dq
                CK[par * 64 + d, m, :Tc] = cos[:, f] * dk
                SK[par * 64 + d, m, :Tc] = sgn * sin[:, f] * dk
    CK0, SK0 = CK.copy(), SK.copy()
    CK0[:, :, :pad_first] = 0.0
    SK0[:, :, :pad_first] = 0.0
    CC = np.zeros((64, 8, 64)); SS = np.zeros((64, 8, 64))
    ang2 = (pos[:, None] - Tc) * freqs[None, :]
    cos2, sin2 = np.cos(ang2), np.sin(ang2)
    for h in range(8):
        dec = np.exp(logg[h] * (Tc - pos))
        for d in range(64):
            f = d % 32
            sgn = -1.0 if d < 32 else 1.0
            CC[:Tc, h, d] = cos2[:, f] * dec
            SS[:Tc, h, d] = sgn * sin2[:, f] * dec
    CC0, SS0 = CC.copy(), SS.copy()
    CC0[:pad_first] = 0.0
    SS0[:pad_first] = 0.0
    CM = np.zeros((64, 8, 64))
    for j in range(Tc):
        CM[j, :, j:Tc] = 1.0
    GT = np.zeros((128, 4))
    for par in range(2):
        for m in range(4):
            GT[par * 64:(par + 1) * 64, m] = np.exp(logg[2 * m + par] * Tc)
    shift = -Tc * freqs
    c, s = np.cos(shift), np.sin(shift)
    R = np.zeros((64, 64))
    for f in range(32):
        R[f, f] = c[f]; R[f, f + 32] = -s[f]
        R[32 + f, 32 + f] = c[f]; R[32 + f, f] = s[f]
    ROT = np.zeros((128, 128))
    ROT[:64, :64] = R.T
    ROT[64:, 64:] = R.T
    import ml_dtypes
    RG = np.zeros((128, 4, 128))
    for m in range(4):
        RG[:, m, :] = ROT * GT[:, m][:, None]
    r32 = RG.reshape(128, 512).astype(np.float32)
    rhi = r32.astype(ml_dtypes.bfloat16).astype(np.float32)
    rlo = (r32 - rhi).astype(ml_dtypes.bfloat16).astype(np.float32)
    ROT = np.concatenate([rhi, rlo], axis=1)
    fm = np.concatenate([CQ, SQ, CK, SK, CK0, SK0], axis=1).reshape(128, 24 * 64)
    tm = np.zeros((128, 5 * 512))
    tm[:64] = np.concatenate([CC, SS, CC0, SS0, CM], axis=1).reshape(64, 5 * 512)
    return (fm.astype(np.float32), tm.astype(np.float32), GT.astype(np.float32), ROT.astype(np.float32))


def _attn_masks():
    M0 = np.full((128, 256), NEG); M1 = np.full((128, 256), NEG); MG = np.full((128, 256), NEG)
    M0[:64, 48:64] = 0.0
    M0[64:, 48:128] = 0.0
    M1[:64, 48:192] = 0.0
    M1[64:, 64:256] = 0.0
    MG[:64, 0:192] = 0.0
    MG[64:, 64:256] = 0.0
    return np.concatenate([M0, M1, MG], axis=1).astype(np.float32)


def _perm_cols():
    qa0, ka0, va0, qr0, kr0, vr0, gr0 = 0, 512, 640, 768, 1280, 1792, 2304
    cols = []
    for m in range(4):
        cols += list(range(qa0 + m * 64, qa0 + (m + 1) * 64)) + list(range(qa0 + (4 + m) * 64, qa0 + (5 + m) * 64))
    cols += list(range(ka0, ka0 + 128))
    for base in (qr0, kr0):
        for m in range(4):
            nat = list(range(base + m * 128, base + (m + 1) * 128))
            per = []
            for hh in range(2):
                b = base + m * 128 + hh * 64
                per += list(range(b + 32, b + 64)) + list(range(b, b + 32))
            cols += nat + per
    cols += list(range(gr0, gr0 + 512))
    assert len(cols) == NEXT_F * 128
    tok = list(range(kr0, kr0 + 512)) + list(range(vr0, vr0 + 512)) + list(range(ka0, ka0 + 128)) + list(range(va0, va0 + 128))
    return np.array(cols), np.array(tok)


def build(cfg):
    NPS, S, NSS, DEPTH = cfg["NPS"], cfg["S"], cfg["NSS"], cfg["DEPTH"]
    TP = S + 64
    TSM = NSS * 32
    TMAX = max(TP, TSM)
    NSUB = (TP + 127) // 128
    nc = bass.Bass("TRN2", target_bir_lowering=False)
    tr = TR()

    def din(name, shape, dt=F32):
        return nc.dram_tensor(name, list(shape), dt, kind="ExternalInput").ap()

    def dout(name, shape):
        return nc.dram_tensor(name, list(shape), F32, kind="ExternalOutput").ap()

    def dscr(name, shape, dt=BF16):
        return nc.dram_tensor(name, list(shape), dt, kind="Internal").ap()

    xp_d = din("xp", [NPS, S, D])
    xs_d = din("xs", [TSM, D])
    ck_d = din("ck", [DEPTH, NSS, 128, 128])
    cv_d = din("cv", [DEPTH, NSS, 128, 128])
    st_d = din("st", [DEPTH, NSS, 8, 64, 64])
    meta_d = din("meta", [16, D])
    w1_d = din("w1", [DEPTH * 2, NJ, 128, KC * 256])
    w2_d = din("w2", [DEPTH * 2, KC, 128, NJ * 128])
    wf_d = din("wf", [DEPTH, NEXT_F, 128, KC * 128])
    wt_d = din("wt", [DEPTH, 128, KC * NTOK])
    wo_d = din("wo", [DEPTH, KC, 128, KC * 128])
    gains_d = din("gains", [128, (DEPTH * 6 + 1) * KC])
    sinks_d = din("sinks", [1, DEPTH * 8])
    tabfm_d = din("tabfm", [2, 128, 24 * 64])
    tabtm_d = din("tabtm", [2, 128, 5 * 512])
    gt_d = din("gt", [2, 128, 4])
    rot_d = din("rot", [2, 128, 1024])
    masks_d = din("masks", [128, 768])
    ident_d = din("ident", [128, 128])

    yp_d = dout("yp", [NPS, S, D])
    ys_d = dout("ys", [TSM, D])
    pk_d = dout("pk", [DEPTH, NPS, 128, 128])
    pv_d = dout("pv", [DEPTH, NPS, 128, 128])
    pr_d = dout("pr", [DEPTH, NPS, 8, 64, 64])
    sk_d = dout("sk", [DEPTH, NSS, 128, 128])
    sv_d = dout("sv", [DEPTH, NSS, 128, 128])
    sr_d = dout("sr", [DEPTH, NSS, 8, 64, 64])

    w1_b = dscr("w1b", [DEPTH * 2, NJ, 128, KC * 256])
    w2_b = dscr("w2b", [DEPTH * 2, KC, 128, NJ * 128])
    wf_b = dscr("wfb", [DEPTH, NEXT_F, 128, KC * 128])
    wt_b = dscr("wtb", [DEPTH, 128, KC * NTOK])
    wo_b = dscr("wob", [DEPTH, KC, 128, KC * 128])
    B_w1 = [[Buf("w1b") for _ in range(NJ)] for _ in range(DEPTH * 2)]
    B_w2 = [[Buf("w2b") for _ in range(KC)] for _ in range(DEPTH * 2)]
    B_wf = [[Buf("wfb") for _ in range(NEXT_F)] for _ in range(DEPTH)]
    B_wt = [Buf("wtb") for _ in range(DEPTH)]
    B_wo = [[Buf("wob") for _ in range(KC)] for _ in range(DEPTH)]

    NTF = 704
    A_BF = 34400
    A_F = 9160
    ctx = []

    def sb(name, shape, dt):
        g = nc.sbuf_tensor(name, list(shape), dt)
        t = g.__enter__()
        ctx.append(g)
        return t

    X = sb("X", [128, KC, TMAX], F32)
    hB = sb("hB", [128, KC, NTF], BF16)
    abf = sb("abf", [128, A_BF], BF16)
    af = sb("af", [128, A_F], F32)
    identF = sb("identF", [128, 128], F32)
    identB = sb("identB", [128, 128], BF16)
    onesB = sb("onesB", [128, 128], BF16)
    rgB = sb("rgB", [128, 1024], BF16)
    bmask = sb("bmask", [128, 128], BF16)
    tabfm = sb("tabfm_s", [128, 24 * 64], F32)
    tabtm = sb("tabtm_s", [128, 5 * 512], F32)
    gtT = sb("gtT", [128, 4], F32)
    masksB = sb("masksB", [128, 768], BF16)
    gains = sb("gains_s", [128, (DEPTH * 6 + 1) * KC], F32)
    sinksT = sb("sinksT", [128, DEPTH * 8], F32)
    gp = nc.psum_tensor("ps", [128, 4096], F32)
    ps = gp.__enter__()
    ctx.append(gp)
    PB = [Buf("bank%d" % i, excl=True) for i in range(8)]

    def bank(i, n=512, off=0):
        return ps[:, i * 512 + off:i * 512 + off + n]

    B_const = Buf("const")
    B_tabs = Buf("tabs")
    B_h = [Buf("h%d" % i) for i in range(4)]
    XB = [Buf("x%d" % i) for i in range((TMAX + 63) // 64)]

    def xb(t0, n):
        return XB[t0 // 64:(t0 + n + 63) // 64]

    class Arena:
        def __init__(self, t, size):
            self.t, self.size, self.off = t, size, 0

        def reset(self):
            self.off = 0

        def take(self, n):
            assert self.off + n <= self.size, (self.off, n, self.size)
            a = self.t[:, self.off:self.off + n]
            self.off += n
            return a

    ABF = Arena(abf, A_BF)
    AFF = Arena(af, A_F)

    def _bank_of(writes):
        return int(writes[0].name[4:])

    def MM(out, lhsT, rhs, start, stop, reads, writes, rows=None):
        tr.op("pe", lambda e: e.matmul(out, lhsT=lhsT, rhs=rhs, start=start, stop=stop), reads, writes,
              pe_bank=_bank_of(writes), pe_rows=rows)

    def TP_(out, in_, ident, reads, writes, rows=None):
        tr.op("pe", lambda e: e.transpose(out, in_, ident), reads, writes, pe_bank=_bank_of(writes), pe_rows=rows)

    def ACT(out, in_, func, reads, writes, bias=None, scale=None, accum_out=None):
        kw = {}
        if bias is not None:
            kw["bias"] = bias
        if scale is not None:
            kw["scale"] = scale
        if accum_out is not None:
            kw["accum_out"] = accum_out
        tr.op("act", lambda e: e.activation(out=out, in_=in_, func=func, **kw), reads, writes)

    def TT(out, in0, in1, op, reads, writes, eng="dve"):
        tr.op(eng, lambda e: e.tensor_tensor(out=out, in0=in0, in1=in1, op=op), reads, writes)

    def TS(out, in0, s1, op0, reads, writes, s2=None, op1=None, eng="dve"):
        if op1 is None:
            tr.op(eng, lambda e: e.tensor_scalar(out=out, in0=in0, scalar1=s1, scalar2=None, op0=op0), reads, writes)
        else:
            tr.op(eng, lambda e: e.tensor_scalar(out=out, in0=in0, scalar1=s1, scalar2=s2, op0=op0, op1=op1), reads, writes)

    def STT(out, in0, scalar, in1, op0, op1, reads, writes, eng="dve"):
        tr.op(eng, lambda e: e.scalar_tensor_tensor(out=out, in0=in0, scalar=scalar, in1=in1, op0=op0, op1=op1), reads, writes)

    def CP(out, in_, reads, writes, eng="dve"):
        tr.op(eng, lambda e: e.tensor_copy(out=out, in_=in_), reads, writes)

    def RECIP(out, in_, reads, writes):
        tr.op("dve", lambda e: e.reciprocal(out=out, in_=in_), reads, writes)

    def MEMSET(ap, val, writes, eng="dve"):
        tr.op(eng, lambda e: e.memset(ap, val), (), writes)

    def DMA(q, out, in_, reads, writes):
        tr.dma(q, lambda e: e.dma_start(out=out, in_=in_), reads, writes)

    def gcol(which, l, kc):
        i = (l * 6 + which) * KC + kc if which < 6 else DEPTH * 6 * KC + kc
        return gains[:, i:i + 1]

    DMA("sp", identF[:], ident_d[:, :], (), (B_const,))
    DMA("sp", gains[:], gains_d[:, :], (), (B_const,))
    DMA("sp", sinksT[:], sinks_d[0:1, :].partition_broadcast(128), (), (B_const,))
    stage = X[:, 0:3, 0:256]
    DMA("sp", stage, masks_d[:, :].rearrange("p (a b) -> p a b", a=3), (), [B_const] + xb(0, 256))
    CP(identB[:], identF[:], (B_const,), (B_const,))
    CP(masksB[:].rearrange("p (a b) -> p a b", a=3), stage, [B_const] + xb(0, 256), (B_const,))
    epsT = sb("epsT", [128, 1], F32)
    MEMSET(epsT[:], EPS, (B_const,), eng="pool")
    MEMSET(onesB[:], 1.0, (B_const,), eng="pool")
    MEMSET(bmask[:], 0.0, (B_const,), eng="pool")
    MEMSET(bmask[0:64, 0:64], 1.0, (B_const,), eng="pool")
    MEMSET(bmask[64:128, 64:128], 1.0, (B_const,), eng="pool")

    def load_tabs(idx):
        DMA("sp", tabfm[:], tabfm_d[idx], (), (B_tabs,))
        DMA("sp", tabtm[:], tabtm_d[idx], (), (B_tabs,))
        DMA("sp", gtT[:], gt_d[idx], (), (B_tabs,))
        stg_ = X[:, 0:4, 0:256]
        DMA("sp", stg_, rot_d[idx].rearrange("p (a b) -> p a b", a=4), (), [B_tabs] + xb(0, 256))
        CP(rgB[:].rearrange("p (a b) -> p a b", a=4), stg_, [B_tabs] + xb(0, 256), (B_tabs,))

    for l in range(DEPTH):
        for f in range(2):
            lf = l * 2 + f
            if f == 1:
                for c in range(NEXT_F):
                    DMA("pool", wf_b[l, c], wf_d[l, c], (), (B_wf[l][c],))
                DMA("pool", wt_b[l], wt_d[l], (), (B_wt[l],))
                for c in range(KC):
                    DMA("pool", wo_b[l, c], wo_d[l, c], (), (B_wo[l][c],))
            for j in range(NJ):
                DMA("pool", w1_b[lf, j], w1_d[lf, j], (), (B_w1[lf][j],))
            for c in range(KC):
                DMA("pool", w2_b[lf, c], w2_d[lf, c], (), (B_w2[lf][c],))

    def rstd_from_sq(sqv, n, ssbank, rt, rstd, rd, wr_bufs, nfeat=1024.0, rt_bufs=None):
        rt_bufs = wr_bufs if rt_bufs is None else rt_bufs
        for kc in range(KC):
            MM(bank(ssbank, n), onesB[:], sqv[:, kc, :], kc == 0, kc == KC - 1, rd + [B_const], [PB[ssbank]])
        ACT(rt, bank(ssbank, n), AF.Sqrt, [PB[ssbank], B_const], rt_bufs, bias=epsT[:, 0:1], scale=1.0 / nfeat)
        RECIP(rstd, rt, rt_bufs, wr_bufs)

    FB = ([Buf("nP0"), Buf("nP1")], [Buf("rt0"), Buf("rt1")], [Buf("w1r") for _ in range(3)],
          [Buf("w2r") for _ in range(2)], [Buf("a0"), Buf("a1")], [Buf("Y0"), Buf("Y1")],
          [Buf("sg0"), Buf("sg1")], [Buf("n0"), Buf("n1")], [Buf("t0"), Buf("t1")])

    def ffn(l, f, T, do_barrier=True):
        lf = l * 2 + f
        ABF.reset(); AFF.reset()
        w1r = [ABF.take(KC * 256).rearrange("p (k c) -> p k c", k=KC) for _ in range(3)]
        w2r = [ABF.take(NJ * 128).rearrange("p (j c) -> p j c", j=NJ) for _ in range(2)]
        aT = ABF.take(NJ * NTF).rearrange("p (j n) -> p j n", j=NJ)
        Y = AFF.take(KC * NTF).rearrange("p (k n) -> p k n", k=KC)
        sg = [AFF.take(352) for _ in range(2)]
        rt = [AFF.take(352) for _ in range(2)]
        rs = [AFF.take(352) for _ in range(2)]
        tmp = [AFF.take(352) for _ in range(2)]
        rsP = [AFF.take(352) for _ in range(2)]
        pending = []
        B_nP, B_rt, B_w1r, B_w2r, B_a, B_Y, B_sg, B_n, B_t = FB
        sts = []
        t = 0
        while t < T:
            n = min(NTF, T - t)
            tiles = []
            o = 0
            while o < n:
                m = min(352, n - o)
                tiles.append((o, m))
                o += m
            sts.append((t, tiles))
            t += n
        cA = 0
        cB = 0
        ring1 = 0
        ring2 = 0
        for (t0, tiles) in sts:
            for ti, (o, n) in enumerate(tiles):
                xs_ = xb(t0 + o, n)
                ACT(hB[:, :, o:o + n], X[:, :, t0 + o:t0 + o + n], AF.Square, xs_, [B_h[ti]])
                rstd_from_sq(hB[:, :, o:o + n], n, 4 + ti, rt[ti][:, :n], rs[ti][:, :n], [B_h[ti]], [B_n[ti]],
                             rt_bufs=[B_rt[ti]])
                for kc in range(KC):
                    STT(hB[:, kc, o:o + n], X[:, kc, t0 + o:t0 + o + n], gcol(0 if f == 0 else 4, l, kc), rs[ti][:, :n],
                        ALU.mult, ALU.mult, xs_ + [B_n[ti], B_const], [B_h[ti]])
            for j in range(NJ):
                s = ring1 % 3
                ring1 += 1
                if "W" not in cfg.get("stages", "") or ring1 <= 3:
                    DMA("sp", w1r[s], w1_b[lf, j].rearrange("p (k c) -> p k c", k=KC), [B_w1[lf][j]], [B_w1r[s]])
                for ti, (o, n) in enumerate(tiles):
                    gb = (2 * cA) % 4
                    ub = (2 * cA + 1) % 4
                    cA += 1
                    for kc in range(KC):
                        MM(bank(gb, n), w1r[s][:, kc, 0:128], hB[:, kc, o:o + n], kc == 0, kc == KC - 1,
                           [B_w1r[s], B_h[ti]], [PB[gb]])
                    for kc in range(KC):
                        MM(bank(ub, n), w1r[s][:, kc, 128:256], hB[:, kc, o:o + n], kc == 0, kc == KC - 1,
                           [B_w1r[s], B_h[ti]], [PB[ub]])
                    ACT(sg[ti][:, :n], bank(gb, n), AF.Silu, [PB[gb]], [B_sg[ti]])
                    TT(aT[:, j, o:o + n], sg[ti][:, :n], bank(ub, n), ALU.mult, [B_sg[ti], PB[ub]], [B_a[ti]])
                    if pending:
                        pending.pop(0)()
            while pending:
                pending.pop(0)()
            for oc in range(KC):
                s = ring2 % 2
                ring2 += 1
                if "W" not in cfg.get("stages", "") or ring2 <= 2:
                    DMA("sp", w2r[s], w2_b[lf, oc].rearrange("p (j c) -> p j c", j=NJ), [B_w2[lf][oc]], [B_w2r[s]])
                for ti, (o, n) in enumerate(tiles):
                    yb = 4 + (cB % 2)
                    cB += 1
                    for j in range(NJ):
                        MM(bank(yb, n), w2r[s][:, j, :], aT[:, j, o:o + n], j == 0, j == NJ - 1,
                           [B_w2r[s], B_a[ti]], [PB[yb]])
                    ACT(Y[:, oc, o:o + n], bank(yb, n), AF.Copy, [PB[yb]], [B_Y[ti]])
                    ACT(hB[:, oc, o:o + n], bank(yb, n), AF.Square, [PB[yb]], [B_h[ti]])
            for ti, (o, n) in enumerate(tiles):
                rstd_from_sq(hB[:, :, o:o + n], n, 6 + ti, rt[ti][:, :n], rsP[ti][:, :n], [B_h[ti]], [B_nP[ti]],
                             rt_bufs=[B_rt[ti]])
            for ti, (o, n) in enumerate(tiles):
                xs_ = xb(t0 + o, n)
                for oc in range(KC):
                    def item(ti=ti, o=o, n=n, oc=oc, xs_=xs_, t0=t0):
                        STT(tmp[ti][:, :n], Y[:, oc, o:o + n], gcol(1 if f == 0 else 5, l, oc), rsP[ti][:, :n],
                            ALU.mult, ALU.mult, [B_Y[ti], B_nP[ti], B_const], [B_t[ti]])
                        STT(X[:, oc, t0 + o:t0 + o + n], tmp[ti][:, :n], 0.5, X[:, oc, t0 + o:t0 + o + n],
                            ALU.mult, ALU.add, [B_t[ti]] + xs_, xs_)
                    pending.append(item)
        while pending:
            pending.pop(0)()
        if do_barrier:
            tr.barrier()

    def mixer(l, T, Tc, prompt, seq):
        ABF.reset(); AFF.reset()
        NT = 256 if prompt else 128
        wtok = ABF.take(KC * NTOK).rearrange("p (k c) -> p k c", k=KC)
        wfr = [ABF.take(KC * 128).rearrange("p (k c) -> p k c", k=KC) for _ in range(3)]
        qaT = ABF.take(4 * NT).rearrange("p (m n) -> p m n", m=4)
        kaT = ABF.take(T)
        Vtok = ABF.take(NSUB * 128).rearrange("p (s c) -> p s c", c=128)
        qtT = ABF.take(4 * NT).rearrange("p (m n) -> p m n", m=4)
        ktT = ABF.take(4 * NT).rearrange("p (m n) -> p m n", m=4)
        gsT = ABF.take(4 * NT).rearrange("p (m n) -> p m n", m=4)
        khat = [ABF.take(512).rearrange("p (h d) -> p h d", h=8) for _ in range(2)]
        vrb = [ABF.take(512).rearrange("p (h d) -> p h d", h=8) for _ in range(2)]
        mixT = ABF.take(KC * NT).rearrange("p (k n) -> p k n", k=KC)
        E = ABF.take(8 * 256).rearrange("p (h s) -> p h s", h=8)
        PT = ABF.take(16 * 128).rearrange("p (s q) -> p s q", s=16)
        On = ABF.take(512)
        STb = [ABF.take(512).rearrange("p (h i) -> p h i", h=8) for _ in range(2)]
        Sbf = ABF.take(512).rearrange("p (m e) -> p m e", m=4)
        onb = [ABF.take(512) for _ in range(2)]
        Slo = ABF.take(512).rearrange("p (m e) -> p m e", m=4)
        cKT = ABF.take(128)
        Vc = ABF.take(128)
        Vn = ABF.take(128)
        Y = AFF.take(KC * NT).rearrange("p (k n) -> p k n", k=KC)
        Sst = AFF.take(512).rearrange("p (m e) -> p m e", m=4)
        Tm = AFF.take(512).rearrange("p (m e) -> p m e", m=4)
        kvf = [AFF.take(256) for _ in range(2)]
        sqo = AFF.take(512)
        rA = AFF.take(512).rearrange("p (h d) -> p h d", h=8)
        rB = AFF.take(512).rearrange("p (h d) -> p h d", h=8)
        t1 = AFF.take(NT)
        t2 = AFF.take(NT)
        cin = [AFF.take(128) for _ in range(2)]
        rt = AFF.take(NT)
        rs = AFF.take(NT)
        tmp = AFF.take(NT)
        stat = AFF.take(64)
        rsP = AFF.take(NT)
        pending = []
        B = {k: Buf(k) for k in ("wtok", "qaT", "kaT", "Vtok", "qtT", "ktT", "gsT", "mixT", "E", "PT", "On", "Sbf",
                                 "cKT", "Vc", "Vn", "Y", "Sst", "Tm", "sqo", "rA", "rB", "t1", "t2", "n", "tmp",
                                 "stat", "h", "rsum", "Th", "Tl", "gn", "rt", "nP", "stat0", "stat1", "rsum0", "rsum1", "E0", "E1", "PT0", "PT1", "On0", "On1")}
        B_wfr = [Buf("wfr") for _ in range(3)]
        B_khat = [Buf("kh0"), Buf("kh1")]
        B_vr = [Buf("vr0"), Buf("vr1")]
        B_STb = [Buf("stb0"), Buf("stb1")]
        B_onb = [Buf("onb0"), Buf("onb1")]
        B_kvf = [Buf("kvf0"), Buf("kvf1")]
        B_cin = [Buf("cin0"), Buf("cin1")]
        ring = [0]

        def wf_load(src_ap, src_buf):
            s = ring[0] % 3
            ring[0] += 1
            DMA("sp", wfr[s], src_ap.rearrange("p (k c) -> p k c", k=KC), [src_buf], [B_wfr[s]])
            return s

        DMA("sp", wtok, wt_b[l].rearrange("p (k c) -> p k c", k=KC), [B_wt[l]], [B["wtok"]])
        def fm(i):
            return tabfm[:, i * 256:(i + 1) * 256].rearrange("p (m i) -> p m i", m=4)

        def tmv(i, rows):
            return tabtm[0:rows, i * 512:(i + 1) * 512].rearrange("p (h d) -> p h d", h=8)

        sink = sinksT[:, l * 8:(l + 1) * 8]
        if prompt:
            MEMSET(Sst[:], 0.0, [B["Sst"]])
            MEMSET(Sbf[:], 0.0, [B["Sbf"]])
            MEMSET(Slo[:], 0.0, [B["Tl"]])
        cnt = {"f": 0, "tok": 0, "rc": 0}

        tiles = []
        t = 0
        while t < T:
            n = min(NT, T - t)
            tiles.append((t, n))
            t += n

        for (t0, n) in tiles:
            xs_ = xb(t0, n)
            nch = n // Tc
            ACT(hB[:, :, 0:n], X[:, :, t0:t0 + n], AF.Square, xs_, [B["h"]])
            rstd_from_sq(hB[:, :, 0:n], n, 7, rt[:, :n], rs[:, :n], [B["h"]], [B["n"]], rt_bufs=[B["rt"]])
            for kc in range(KC):
                STT(hB[:, kc, 0:n], X[:, kc, t0:t0 + n], gcol(2, l, kc), rs[:, :n], ALU.mult, ALU.mult,
                    xs_ + [B["n"], B_const], [B["h"]])

            def proj(c):
                s = wf_load(wf_b[l, c], B_wf[l][c])
                k = cnt["f"] % 8
                cnt["f"] += 1
                b, off = k // 2, (k % 2) * 256
                for kc in range(KC):
                    MM(bank(b, n, off), wfr[s][:, kc, :], hB[:, kc, 0:n], kc == 0, kc == KC - 1,
                       [B_wfr[s], B["h"]], [PB[b]])
                if pending:
                    pending.pop(0)()
                return b, off

            stg = cfg.get("stages", "xfmy") + ("paro" if not any(c in cfg.get("stages", "xfmy") for c in "paro") else "")
            if "p" in stg and not any(c in stg for c in "123"):
                stg = stg + "123"
            for m in (range(4) if "1" in stg else []):
                b, off = proj(m)
                ACT(qaT[:, m, 0:n], bank(b, n, off), AF.Copy, [PB[b]], [B["qaT"]])
            if "1" in stg:
                b, off = proj(4)
                ACT(kaT[:, t0:t0 + n], bank(b, n, off), AF.Copy, [PB[b]], [B["kaT"]])
            for (cbase, dst, dbuf, ci, si) in (((5, qtT, "qtT", 0, 1), (13, ktT, "ktT", 2, 3)) if "2" in stg else []):
                for m in range(4):
                    b1, o1 = proj(cbase + 2 * m)
                    b2, o2 = proj(cbase + 2 * m + 1)
                    v3 = lambda a: a.rearrange("p (c i) -> p c i", i=Tc)
                    TT(v3(t1[:, :n]), v3(bank(b1, n, o1)), fm(ci)[:, m, 0:Tc].unsqueeze(1).to_broadcast([128, nch, Tc]),
                       ALU.mult, [PB[b1], B_tabs], [B["t1"]])
                    TT(v3(t2[:, :n]), v3(bank(b2, n, o2)), fm(si)[:, m, 0:Tc].unsqueeze(1).to_broadcast([128, nch, Tc]),
                       ALU.mult, [PB[b2], B_tabs], [B["t2"]])
                    TT(dst[:, m, 0:n], t1[:, :n], t2[:, :n], ALU.add, [B["t1"], B["t2"]], [B[dbuf]])
                if dbuf == "ktT" and prompt and t0 == 0:
                    MEMSET(ktT[:, :, 0:48], 0.0, [B["ktT"]])
            for m in (range(4) if "3" in stg else []):
                b, off = proj(21 + m)
                ACT(gsT[:, m, 0:n], bank(b, n, off), AF.Silu, [PB[b]], [B["gsT"]])

            def r_front(ch):
                tcs = t0 + ch * Tc
                o_in = ch * Tc
                slot = cnt["rc"] % 2
                cnt["rc"] += 1
                first = prompt and tcs == 0
                for gi in range(2):
                    for kc in range(KC):
                        MM(ps[0:Tc, (4 + gi) * 512:(5 + gi) * 512], hB[:, kc, o_in:o_in + Tc],
                           wtok[:, kc, gi * 512:(gi + 1) * 512], kc == 0, kc == KC - 1,
                           [B["h"], B["wtok"]], [PB[4 + gi]])
                kps = ps[0:Tc, 4 * 512:5 * 512].rearrange("p (h d) -> p h d", h=8)
                yield
                TT(rA[0:Tc], kps, tmv(0, Tc), ALU.mult, [PB[4], B_tabs], [B["rA"]])
                TT(rB[0:Tc, :, 0:32], kps[:, :, 32:64], tmv(1, Tc)[:, :, 0:32], ALU.mult, [PB[4], B_tabs], [B["rB"]])
                TT(rB[0:Tc, :, 32:64], kps[:, :, 0:32], tmv(1, Tc)[:, :, 32:64], ALU.mult, [PB[4], B_tabs], [B["rB"]])
                TT(khat[slot][0:Tc], rA[0:Tc], rB[0:Tc], ALU.add, [B["rA"], B["rB"]], [B_khat[slot]])
                if first:
                    MEMSET(khat[slot][0:48], 0.0, [B_khat[slot]])
                ACT(vrb[slot][0:Tc], ps[0:Tc, 5 * 512:6 * 512].rearrange("p (h d) -> p h d", h=8), AF.Copy,
                    [PB[5]], [B_vr[slot]])
                yield
                for h in range(8):
                    m, par = h // 2, h % 2
                    MM(ps[0:Tc, par * 512 + m * 64:par * 512 + m * 64 + Tc],
                       ktT[par * 64:(par + 1) * 64, m, o_in:o_in + Tc],
                       qtT[par * 64:(par + 1) * 64, m, o_in:o_in + Tc], True, True, [B["ktT"], B["qtT"]], [PB[par]],
                       rows=(par * 64, 64))
                yield
                for par in range(2):
                    TT(STb[slot][0:Tc, par:8:2, 0:Tc],
                       ps[0:Tc, par * 512:par * 512 + 256].rearrange("p (h i) -> p h i", h=4)[:, :, 0:Tc],
                       tmv(4, Tc)[:, 0:4, 0:Tc], ALU.mult, [PB[par], B_tabs], [B_STb[slot]])
                yield
                return slot

            def r_back(ch, slot):
                tcs = t0 + ch * Tc
                o_in = ch * Tc
                if not prompt:
                    sq_i = tcs // 32
                    MEMSET(Sst[:], 0.0, [B["Sst"]])
                    for par in range(2):
                        DMA("sp", Sst[par * 64:(par + 1) * 64, :, par * 64:(par + 1) * 64],
                            st_d[l, sq_i].rearrange("(m r) d e -> r d m e", r=2)[par], [], [B["Sst"]])
                    CP(Sbf[:], Sst[:], [B["Sst"]], [B["Sbf"]])
                    TT(Slo[:], Sst[:], Sbf[:], ALU.subtract, [B["Sst"], B["Sbf"]], [B["Tl"]])
                for m in range(4):
                    MM(ps[:, 7 * 512 + m * 128:7 * 512 + (m + 1) * 128],
                       khat[slot][0:Tc, 2 * m:2 * m + 2, :].rearrange("p h d -> p (h d)"),
                       vrb[slot][0:Tc, 2 * m:2 * m + 2, :].rearrange("p h d -> p (h d)"), m == 0, False,
                       [B_khat[slot], B_vr[slot]], [PB[7]], rows=(0, Tc))
                for m in range(4):
                    reg = ps[:, 7 * 512 + m * 128:7 * 512 + (m + 1) * 128]
                    MM(reg, rgB[:, m * 128:(m + 1) * 128], Sbf[:, m, :], False, False, [B["Sbf"], B_tabs], [PB[7]])
                    MM(reg, rgB[:, m * 128:(m + 1) * 128], Slo[:, m, :], False, False, [B["Tl"], B_tabs], [PB[7]])
                    MM(reg, rgB[:, 512 + m * 128:512 + (m + 1) * 128], Sbf[:, m, :], False, m == 3,
                       [B["Sbf"], B_tabs], [PB[7]])
                for h in range(8):
                    MM(ps[0:Tc, 2 * 512 + h * 64:2 * 512 + (h + 1) * 64], STb[slot][0:Tc, h, 0:Tc],
                       vrb[slot][0:Tc, h, :], h == 0, False, [B_STb[slot], B_vr[slot]], [PB[2]], rows=(0, Tc))
                for m in range(4):
                    MM(ps[0:Tc, 2 * 512 + m * 128:2 * 512 + (m + 1) * 128], qtT[:, m, o_in:o_in + Tc],
                       Sbf[:, m, :], False, m == 3, [B["qtT"], B["Sbf"]], [PB[2]])
                yield
                TT(Sst[:], ps[:, 7 * 512:8 * 512].rearrange("p (m e) -> p m e", m=4),
                   bmask[:].unsqueeze(1).to_broadcast([128, 4, 128]), ALU.mult, [PB[7], B_const], [B["Sst"]])
                CP(Sbf[:], Sst[:], [B["Sst"]], [B["Sbf"]])
                TT(Slo[:], Sst[:], Sbf[:], ALU.subtract, [B["Sst"], B["Sbf"]], [B["Tl"]])
                yield
                ACT(sqo[0:Tc, :], ps[0:Tc, 1024:1536], AF.Square, [PB[2]], [B["sqo"]])
                gn = stat[0:Tc, 56:64]
                tr.op("dve", lambda e, o=gn, i=sqo[0:Tc, :].rearrange("p (h d) -> p h d", h=8):
                      e.tensor_reduce(out=o, in_=i, axis=AX.X, op=ALU.add), [B["sqo"]], [B["gn"]])
                yield
                ACT(gn, gn, AF.Sqrt, [B["gn"], B_const], [B["gn"]], bias=epsT[0:Tc, 0:1], scale=1.0 / 64.0)
                RECIP(gn, gn, [B["gn"]], [B["gn"]])
                TT(onb[slot][0:Tc, :].rearrange("p (h d) -> p h d", h=8),
                   ps[0:Tc, 1024:1536].rearrange("p (h d) -> p h d", h=8),
                   gn.unsqueeze(2).to_broadcast([Tc, 8, 64]), ALU.mult, [PB[2], B["gn"]], [B_onb[slot]])
                yield
                psb2 = ps[:, 3 * 512:4 * 512].bitcast(BF16).rearrange("p (m q) -> p m q", m=8)
                for m in range(4):
                    TP_(psb2[:, m, 0:Tc], onb[slot][0:Tc, m * 128:(m + 1) * 128], identB[0:Tc, 0:Tc],
                        [B_onb[slot], B_const], [PB[3]], rows=(0, Tc))
                yield
                TT(mixT[:, 4:8, o_in:o_in + Tc], psb2[:, 0:4, 0:Tc], gsT[:, :, o_in:o_in + Tc], ALU.mult,
                   [PB[3], B["gsT"]], [B["mixT"]])
                last = (tcs + Tc >= T) if prompt else True
                if last:
                    dst = pr_d[l, seq] if prompt else sr_d[l, tcs // 32]
                    for par in range(2):
                        DMA("pool", dst.rearrange("(m r) d e -> r d m e", r=2)[par],
                            Sst[par * 64:(par + 1) * 64, :, par * 64:(par + 1) * 64], [B["Sst"]], [])

            def attn_half(hs, bs, hf, ta, nq, o_in, segs, mask, rdK):
                g = hs // 4
                nk = sum(sg_[2] for sg_ in segs)
                hsl = slice(hs, hs + 4)
                Bst, Brs, BE, BPT, BOn = B["stat%d" % hf], B["rsum%d" % hf], B["E%d" % hf], B["PT%d" % hf], B["On%d" % hf]
                for hh in range(4):
                    m = hh
                    b = bs + hh // 2
                    col = (hh % 2) * 256
                    if mask is not None:
                        MM(ps[0:nq, b * 512 + col:b * 512 + col + nk], identB[0:nq, 0:nq], mask[0:nq, 0:nk],
                           True, False, [B_const], [PB[b]], rows=(0, nq))
                    k0 = 0
                    for si_, (kt_ap, v_ap, kn) in enumerate(segs):
                        MM(ps[0:nq, b * 512 + col + k0:b * 512 + col + k0 + kn],
                           qaT[g * 64:(g + 1) * 64, m, o_in:o_in + nq], kt_ap[g * 64:(g + 1) * 64, 0:kn],
                           mask is None, (mask is None) or si_ == len(segs) - 1, [B["qaT"]] + rdK, [PB[b]],
                           rows=(g * 64, 64))
                        k0 += kn
                yield
                Sv = ps[0:nq, bs * 512:bs * 512 + 1024].rearrange("p (h s) -> p h s", h=4)
                mx, mx2, nb, rsum = stat[0:nq, 0:8][:, hsl], stat[0:nq, 8:16][:, hsl], stat[0:nq, 16:24][:, hsl], stat[0:nq, 24:32][:, hsl]
                sk_, esk, rinv = stat[0:nq, 32:40][:, hsl], stat[0:nq, 40:48][:, hsl], stat[0:nq, 48:56][:, hsl]
                tr.op("dve", lambda e, o=mx, i=Sv[:, :, 0:nk]: e.tensor_reduce(out=o, in_=i, axis=AX.X, op=ALU.max),
                      PB[bs:bs + 2], [Bst])
                STT(mx2, mx, 0.125, sink[0:nq, hsl], ALU.mult, ALU.max, [Bst, B_const], [Bst])
                TS(nb, mx2, -1.0, ALU.mult, [Bst], [Bst])
                TT(sk_, sink[0:nq, hsl], mx2, ALU.subtract, [Bst, B_const], [Bst])
                yield
                for hh in range(4):
                    ACT(E[0:nq, hs + hh, 0:nk], Sv[:, hh, 0:nk], AF.Exp, [PB[bs + hh // 2], Bst], [BE, Brs],
                        bias=nb[:, hh:hh + 1], scale=0.125, accum_out=rsum[:, hh:hh + 1])
                ACT(esk, sk_, AF.Exp, [Bst], [Bst])
                yield
                psb = ps[:, (bs + 2) * 512:(bs + 3) * 512].bitcast(BF16).rearrange("p (s q) -> p s q", s=8)
                for hh in range(4):
                    k0 = 0
                    for si_, (kt_ap, v_ap, kn) in enumerate(segs):
                        TP_(psb[0:kn, hh * 2 + si_, 0:nq], E[0:nq, hs + hh, k0:k0 + kn], identB[0:nq, 0:nq],
                            [BE, B_const], [PB[bs + 2]], rows=(0, nq))
                        k0 += kn
                TT(esk, esk, rsum, ALU.add, [Bst, Brs], [Bst])
                RECIP(rinv, esk, [Bst], [Bst])
                yield
                for si_, (kt_ap, v_ap, kn) in enumerate(segs):
                    ACT(PT[0:kn, hs * 2 + si_:hs * 2 + 8:2, 0:nq], psb[0:kn, si_:8:2, 0:nq], AF.Copy,
                        [PB[bs + 2]], [BPT])
                yield
                ob = (bs + 3) * 512
                for hh in range(4):
                    for si_, (kt_ap, v_ap, kn) in enumerate(segs):
                        MM(ps[0:nq, ob + hh * 64:ob + (hh + 1) * 64], PT[0:kn, (hs + hh) * 2 + si_, 0:nq],
                           v_ap[0:kn, g * 64:(g + 1) * 64], si_ == 0, si_ == len(segs) - 1,
                           [BPT] + rdK, [PB[bs + 3]], rows=(0, kn))
                yield
                TT(On[0:nq, hs * 64:(hs + 4) * 64].rearrange("p (h d) -> p h d", h=4),
                   ps[0:nq, ob:ob + 256].rearrange("p (h d) -> p h d", h=4),
                   rinv.unsqueeze(2).to_broadcast([nq, 4, 64]), ALU.mult, [PB[bs + 3], Bst], [BOn])
                yield
                psbT = ps[:, ob + 256:ob + 512].bitcast(BF16).rearrange("p (m q) -> p m q", m=4)
                for mm in range(2):
                    mfeat = hs // 2 + mm
                    TP_(psbT[:, mm, 0:nq], On[0:nq, mfeat * 128:(mfeat + 1) * 128], identB[0:nq, 0:nq],
                        [BOn, B_const], [PB[bs + 3]], rows=(0, nq))
                yield
                ACT(mixT[:, hs // 2:hs // 2 + 2, o_in:o_in + nq], psbT[:, 0:2, 0:nq], AF.Copy,
                    [PB[bs + 3]], [B["mixT"]])
                yield

            def attn_all():
                au_gens = []
                if prompt:
                    aus = [(t0 + o, min(128, n - o)) for o in range(0, n, 128)]
                else:
                    aus = [(t0 + o, 32) for o in range(0, n, 32)]
                for (ta, nq) in aus:
                    o_in = ta - t0
                    sub = ta // 128
                    kslot = cnt["tok"] % 2
                    cnt["tok"] += 1
                    for kc in range(KC):
                        MM(ps[0:nq, 3 * 512:3 * 512 + 256], hB[:, kc, o_in:o_in + nq], wtok[:, kc, 1024:1280],
                           kc == 0, kc == KC - 1, [B["h"], B["wtok"]], [PB[3]])
                    ACT(kvf[kslot][0:nq, :], ps[0:nq, 3 * 512:3 * 512 + 256], AF.Copy, [PB[3]], [B_kvf[kslot]])
                    if prompt:
                        CP(Vtok[0:nq, sub, :], kvf[kslot][0:nq, 128:256], [B_kvf[kslot]], [B["Vtok"]])
                        lo = max(ta, T - 128)
                        hi = ta + nq
                        if hi > lo:
                            r0 = lo - (T - 128)
                            DMA("pool", pk_d[l, seq, r0:r0 + hi - lo, :], kvf[kslot][lo - ta:hi - ta, 0:128],
                                [B_kvf[kslot]], [])
                            DMA("pool", pv_d[l, seq, r0:r0 + hi - lo, :], kvf[kslot][lo - ta:hi - ta, 128:256],
                                [B_kvf[kslot]], [])
                        segs = []
                        if sub > 0:
                            segs.append((kaT[:, ta - 128:ta], Vtok[:, sub - 1, :], 128))
                        segs.append((kaT[:, ta:ta + nq], Vtok[:, sub, :], nq))
                        if sub == 0:
                            mask = masksB[:, 0:256]
                        elif sub == 1:
                            mask = masksB[:, 256:512]
                        else:
                            mask = masksB[:, 512:768]
                        if nq < 128:
                            assert sub >= 2
                            mask = None
                        rdK = [B["kaT"], B["Vtok"]]
                    else:
                        sq_i = ta // 32
                        CP(Vn[0:32, :], kvf[kslot][0:32, 128:256], [B_kvf[kslot]], [B["Vn"]])
                        DMA("sp", cin[0][:, :], ck_d[l, sq_i], [], [B_cin[0]])
                        DMA("sp", cin[1][:, :], cv_d[l, sq_i], [], [B_cin[1]])
                        TP_(bank(3, 128, 256), cin[0][:, :], identF[:], [B_cin[0], B_const], [PB[3]])
                        ACT(cKT[:, :], bank(3, 128, 256), AF.Copy, [PB[3]], [B["cKT"]])
                        CP(Vc[:, :], cin[1][:, :], [B_cin[1]], [B["Vc"]])
                        DMA("pool", sk_d[l, sq_i, 0:96, :], ck_d[l, sq_i, 32:128, :], [], [])
                        DMA("pool", sv_d[l, sq_i, 0:96, :], cv_d[l, sq_i, 32:128, :], [], [])
                        DMA("pool", sk_d[l, sq_i, 96:128, :], kvf[kslot][0:32, 0:128], [B_kvf[kslot]], [])
                        DMA("pool", sv_d[l, sq_i, 96:128, :], kvf[kslot][0:32, 128:256], [B_kvf[kslot]], [])
                        segs = [(cKT[:, :], Vc[:, :], 128), (kaT[:, ta:ta + 32], Vn[:, :], 32)]
                        mask = None
                        rdK = [B["kaT"], B["cKT"], B["Vc"], B["Vn"]]
                    g0 = attn_half(0, 0, 0, ta, nq, o_in, segs, mask, rdK)
                    g1 = attn_half(4, 4, 1, ta, nq, o_in, segs, mask, rdK)
                    if prompt:
                        au_gens.append((g0, g1))
                    else:
                        yield
                        for _ in zip(g0, g1):
                            yield
                if au_gens:
                    SK = 3
                    live = []
                    step = 0
                    nxt = 0
                    while nxt < len(au_gens) or live:
                        if nxt < len(au_gens) and step % SK == 0 and len(live) < 2 + 1:
                            live.append(au_gens[nxt])
                            nxt += 1
                        for pair in list(live):
                            done = False
                            for g in pair:
                                try:
                                    next(g)
                                except StopIteration:
                                    done = True
                            if done:
                                live.remove(pair)
                        step += 1
                        yield
            def ret_pipeline():
                nslot = yield from r_front(0)
                for ch in range(nch):
                    cur = nslot
                    if ch + 1 < nch:
                        nslot = yield from r_front(ch + 1)
                    yield from r_back(ch, cur)

            for _ in attn_all():
                pass
            for _ in ret_pipeline():
                pass

            for oc in (range(KC) if "o" in stg else []):
                s = wf_load(wo_b[l, oc], B_wo[l][oc])
                yb = cnt["f"] % 8
                cnt["f"] += 1
                b, off = yb // 2, (yb % 2) * 256
                for kc in range(KC):
                    MM(bank(b, n, off), wfr[s][:, kc, :], mixT[:, kc, 0:n], kc == 0, kc == KC - 1,
                       [B_wfr[s], B["mixT"]], [PB[b]])
                ACT(Y[:, oc, 0:n], bank(b, n, off), AF.Copy, [PB[b]], [B["Y"]])
                ACT(hB[:, oc, 0:n], bank(b, n, off), AF.Square, [PB[b]], [B["h"]])
            while pending:
                pending.pop(0)()
            rstd_from_sq(hB[:, :, 0:n], n, 7, rt[:, :n], rsP[:, :n], [B["h"]], [B["nP"]], rt_bufs=[B["rt"]])
            for oc in range(KC):
                def item(oc=oc, n=n, t0=t0, xs_=xs_):
                    STT(tmp[:, :n], Y[:, oc, 0:n], gcol(3, l, oc), rsP[:, :n], ALU.mult, ALU.mult,
                        [B["Y"], B["nP"], B_const], [B["tmp"]])
                    TT(X[:, oc, t0:t0 + n], tmp[:, :n], X[:, oc, t0:t0 + n], ALU.add, [B["tmp"]] + xs_, xs_)
                pending.append(item)
        while pending:
            pending.pop(0)()
        tr.barrier()

    def load_x(T, prompt, seq):
        AFF.reset()
        xin = [AFF.take(1024) for _ in range(4)]
        B_xin = [Buf("xin%d" % i) for i in range(4)]
        k = 0
        if prompt:
            MEMSET(X[:, :, 0:48], 0.0, xb(0, 48))
            units = [(48, 16, meta_d[:, :])] + [(64 + i * 128, 128, xp_d[seq, i * 128:(i + 1) * 128, :])
                                               for i in range(S // 128)]
        else:
            units = [(0, T, xs_d[:, :])]
        for (t0, n, src) in units:
            s = k % 4
            DMA("sp", xin[s][0:n, :], src, [], [B_xin[s]])
            for half in range(2):
                b = 2 * (k % 2) + half
                for q in range(4):
                    kc = half * 4 + q
                    TP_(ps[:, b * 512 + q * 128:b * 512 + q * 128 + n], xin[s][0:n, kc * 128:(kc + 1) * 128],
                        identF[0:n, 0:n], [B_xin[s], B_const], [PB[b]], rows=(0, n))
                ACT(X[:, half * 4:half * 4 + 4, t0:t0 + n],
                    ps[:, b * 512:(b + 1) * 512].rearrange("p (q n) -> p q n", q=4)[:, :, 0:n], AF.Copy,
                    [PB[b]], xb(t0, n))
            k += 1
        tr.barrier()

    def store_y(T, prompt, seq):
        AFF.reset()
        yn = [AFF.take(1024).rearrange("p (k n) -> p k n", k=KC) for _ in range(2)]
        yo = [AFF.take(1024) for _ in range(2)]
        rt2 = [AFF.take(128) for _ in range(2)]
        rs2 = [AFF.take(128) for _ in range(2)]
        B_yn = [Buf("yn0"), Buf("yn1")]
        B_yo = [Buf("yo0"), Buf("yo1")]
        B_n2 = [Buf("n0"), Buf("n1")]
        B_hh2 = [Buf("h0"), Buf("h1")]
        if prompt:
            units = [(64 + i * 128, 128, yp_d[seq, i * 128:(i + 1) * 128, :]) for i in range(S // 128)]
        else:
            units = [(0, T, ys_d[:, :])]
        for k, (t0, n, dst) in enumerate(units):
            s = k % 2
            xs_ = xb(t0, n)
            hv = hB[:, :, s * 128:s * 128 + n]
            ACT(hv, X[:, :, t0:t0 + n], AF.Square, xs_, [B_hh2[s]])
            rstd_from_sq(hv, n, 6 + s, rt2[s][:, :n], rs2[s][:, :n], [B_hh2[s]], [B_n2[s]])
            for kc in range(KC):
                STT(yn[s][:, kc, 0:n], X[:, kc, t0:t0 + n], gcol(6, 0, kc), rs2[s][:, :n], ALU.mult, ALU.mult,
                    xs_ + [B_n2[s], B_const], [B_yn[s]])
            for half in range(2):
                b = 2 * (k % 2) + half
                for q in range(4):
                    kc = half * 4 + q
                    TP_(ps[0:n, b * 512 + q * 128:b * 512 + (q + 1) * 128], yn[s][:, kc, 0:n], identF[:, :],
                        [B_yn[s], B_const], [PB[b]])
                ACT(yo[s][0:n, half * 512:(half + 1) * 512], ps[0:n, b * 512:(b + 1) * 512], AF.Copy, [PB[b]], [B_yo[s]])
            DMA("pool", dst, yo[s][0:n, :], [B_yo[s]], [])
        tr.barrier()

    load_tabs(0)
    passes = [("p", i) for i in range(NPS)] + ([("s", 0)] if NSS > 0 else [])
    for (kind, seq) in passes:
        prompt = kind == "p"
        T = TP if prompt else TSM
        Tc = 64 if prompt else 32
        if not prompt:
            tr.barrier()
            load_tabs(1)
        stg = cfg.get("stages", "xfmy")
        if "x" in stg:
            load_x(T, prompt, seq)
        for l in range(DEPTH):
            if "f" in stg:
                ffn(l, 0, T)
            if "m" in stg:
                mixer(l, T, Tc, prompt, seq)
            if "f" in stg:
                ffn(l, 1, T, do_barrier=(l == DEPTH - 1) or ("m" not in stg and False))
        if "y" in stg:
            store_y(T, prompt, seq)
    tr.final_wait()

    sem_ctx = {}
    sems = {}
    for name in tr.sem_names():
        g = nc.semaphore("s_" + name.replace(":", "_"))
        sems[name] = g.__enter__()
        sem_ctx[name] = g
    with nc.Block() as block:
        @block.sync
        def _(e):
            tr.replay("sp", e, sems)

        @block.scalar
        def _(e):
            tr.replay("act", e, sems)

        @block.vector
        def _(e):
            tr.replay("dve", e, sems)

        @block.gpsimd
        def _(e):
            tr.replay("pool", e, sems)

        @block.tensor
        def _(e):
            tr.replay("pe", e, sems)
    for g in sem_ctx.values():
        g.__exit__(None, None, None)
    for g in reversed(ctx):
        g.__exit__(None, None, None)
    ninst = {k: len(v) for k, v in tr.ops.items()}
    return nc, ninst


def prep_shared(inp, DEPTH):
    f32 = np.float32
    w1 = np.empty((DEPTH * 2, NJ, 128, KC * 256), f32)
    w2 = np.empty((DEPTH * 2, KC, 128, NJ * 128), f32)
    for l in range(DEPTH):
        for f, (wi, wo_) in enumerate((("ffn1_w_in", "ffn1_w_out"), ("ffn2_w_in", "ffn2_w_out"))):
            a = np.asarray(inp[wi][l], f32).reshape(KC, 128, 2, NJ, 128)
            w1[l * 2 + f] = a.transpose(3, 1, 0, 2, 4).reshape(NJ, 128, KC * 256)
            b = np.asarray(inp[wo_][l], f32).reshape(NJ, 128, KC, 128)
            w2[l * 2 + f] = b.transpose(2, 1, 0, 3).reshape(KC, 128, NJ * 128)
    cols, tok = _perm_cols()
    wf = np.empty((DEPTH, NEXT_F, 128, KC * 128), f32)
    wt = np.empty((DEPTH, 128, KC * NTOK), f32)
    wo = np.empty((DEPTH, KC, 128, KC * 128), f32)
    for l in range(DEPTH):
        w = np.asarray(inp["w_in"][l], f32)
        a = w[:, cols].reshape(KC, 128, NEXT_F, 128)
        wf[l] = a.transpose(2, 1, 0, 3).reshape(NEXT_F, 128, KC * 128)
        t = w[:, tok].reshape(KC, 128, NTOK)
        wt[l] = t.transpose(1, 0, 2).reshape(128, KC * NTOK)
        o = np.asarray(inp["w_out"][l], f32).reshape(KC, 128, KC, 128)
        wo[l] = o.transpose(2, 1, 0, 3).reshape(KC, 128, KC * 128)
    gl = []
    for l in range(DEPTH):
        for nm in ("norm_ffn1_pre", "norm_ffn1_post", "norm_mix_pre", "norm_mix_post", "norm_ffn2_pre", "norm_ffn2_post"):
            gl.append(np.asarray(inp[nm][l], f32).reshape(KC, 128).T)
    gl.append(np.asarray(inp["final_norm"], f32).reshape(KC, 128).T)
    gains = np.concatenate(gl, axis=1)
    sinks = np.asarray(inp["attn_sinks"], f32).reshape(1, DEPTH * 8)
    t64 = _tables(64, 48)
    t32 = _tables(32, 0)
    shared = {
        "w1": w1, "w2": w2, "wf": wf, "wt": wt, "wo": wo, "gains": np.ascontiguousarray(gains), "sinks": sinks,
        "tabfm": np.stack([t64[0], t32[0]]), "tabtm": np.stack([t64[1], t32[1]]),
        "gt": np.stack([t64[2], t32[2]]), "rot": np.stack([t64[3], t32[3]]),
        "masks": _attn_masks(), "ident": np.eye(128, dtype=f32),
        "meta": np.asarray(inp["meta_tokens"], f32),
    }
    return shared


_CACHE = {}


def run(inp, cfg, n_cores):
    NPS, S, NSS, DEPTH = cfg["NPS"], cfg["S"], cfg["NSS"], cfg["DEPTH"]
    key = (NPS, S, NSS, DEPTH, cfg.get("stages", "xfmy"))
    if key not in _CACHE:
        _CACHE[key] = build(cfg)
    nc, ninst = _CACHE[key]
    shared = prep_shared(inp, DEPTH)
    f32 = np.float32
    in_maps = []
    for c in range(n_cores):
        m = dict(shared)
        m["xp"] = np.ascontiguousarray(np.asarray(inp["x_prompt"], f32)[c * NPS:(c + 1) * NPS])
        m["xs"] = np.ascontiguousarray(np.asarray(inp["x_sample"], f32)[c * NSS:(c + 1) * NSS]).reshape(NSS * 32, D)
        m["ck"] = np.ascontiguousarray(np.asarray(inp["cache_swa_k"], f32)[:, c * NSS:(c + 1) * NSS]).reshape(DEPTH, NSS, 128, 128)
        m["cv"] = np.ascontiguousarray(np.asarray(inp["cache_swa_v"], f32)[:, c * NSS:(c + 1) * NSS]).reshape(DEPTH, NSS, 128, 128)
        m["st"] = np.ascontiguousarray(np.asarray(inp["state_ret"], f32)[:, c * NSS:(c + 1) * NSS])
        in_maps.append(m)
    res = run_bass_kernel_spmd(nc, in_maps, core_ids=list(range(n_cores)))
    R = res.results
    yp = np.concatenate([r["yp"] for r in R], axis=0)
    ys = np.concatenate([r["ys"].reshape(NSS, 32, D) for r in R], axis=0)
    pk = np.concatenate([r["pk"].reshape(DEPTH, NPS, 128, 2, 64) for r in R], axis=1)
    pv = np.concatenate([r["pv"].reshape(DEPTH, NPS, 128, 2, 64) for r in R], axis=1)
    pr = np.concatenate([r["pr"] for r in R], axis=1)
    sk = np.concatenate([r["sk"].reshape(DEPTH, NSS, 128, 2, 64) for r in R], axis=1)
    sv = np.concatenate([r["sv"].reshape(DEPTH, NSS, 128, 2, 64) for r in R], axis=1)
    sr = np.concatenate([r["sr"] for r in R], axis=1)
    return (yp, ys, pk, pv, pr, sk, sv, sr)


def kernel(**inputs):
    cfg = {"NPS": 4, "S": 2048, "NSS": 4, "DEPTH": 4}
    return run(inputs, cfg, 8)
```

```python
import bisect
from itertools import zip_longest
import numpy as np
import concourse.bass as bass
import concourse.mybir as mybir
from concourse.bass_utils import run_bass_kernel_spmd

F32 = mybir.dt.float32
BF16 = mybir.dt.bfloat16
AF = mybir.ActivationFunctionType
ALU = mybir.AluOpType
AX = mybir.AxisListType

D = 1024
KC = 8
DFF = 2816
NJ = 22
EPS = 1e-6
NEXT_F = 25
NTOK = 1280
NEG = -30000.0


class Buf:
    __slots__ = ("name", "w", "r", "excl")

    def __init__(self, name, excl=False):
        self.name = name
        self.w = None
        self.r = {}
        self.excl = excl


class TR:
    CE = ("pe", "act", "dve", "pool")
    QE = ("sp", "pool")

    def __init__(self, nslots=8):
        self.ops = {e: [] for e in ("pe", "act", "dve", "pool", "sp")}
        self.serial = {e: 0 for e in self.CE}
        self.cnt = {e: 0 for e in self.CE}
        self.sig = {e: [] for e in self.CE}
        self.sigser = {e: [] for e in self.CE}
        self.seen = {e: {} for e in self.ops}
        self.nslots = nslots
        self.pe_last = {}
        self.dcount = {}
        self.dnext = {q: 0 for q in self.QE}
        for q in self.QE:
            for k in range(nslots):
                self.dcount["d:%s%d" % (q, k)] = 0
        self.dnext["cast"] = 0
        for k in range(nslots):
            self.dcount["d:cast%d" % k] = 0

    def sem_names(self):
        return list(self.CE) + list(self.dcount.keys())

    def _signal_upto(self, e, s):
        i = bisect.bisect_left(self.sigser[e], s)
        if i < len(self.sigser[e]):
            return self.sig[e][i]
        last = None
        for o in reversed(self.ops[e]):
            if o["kind"] == "c":
                last = o
                break
        assert last is not None and last["sig"] is None
        self.cnt[e] += 1
        last["sig"] = (e, 1)
        self.sigser[e].append(self.serial[e])
        self.sig[e].append(self.cnt[e])
        return self.cnt[e]

    def _wait(self, me, e, s):
        if e.startswith("d:"):
            c = s
        else:
            if e == me and e == "pe":
                return
            c = self._signal_upto(e, s)
        if self.seen[me].get(e, 0) >= c:
            return
        self.seen[me][e] = c
        self.ops[me].append({"kind": "w", "sem": e, "val": c})

    def _deps(self, me, reads, writes):
        deps = set()
        for b in reads:
            if b.w is not None:
                deps.add(b.w)
            if b.excl:
                for e, s in b.r.items():
                    if e != me:
                        deps.add((e, s))
        for b in writes:
            if b.w is not None:
                deps.add(b.w)
            for e, s in b.r.items():
                deps.add((e, s))
        for e, s in sorted(deps):
            self._wait(me, e, s)

    def op(self, eng, fn, reads=(), writes=(), pe_bank=None, pe_rows=None):
        if eng == "pe" and pe_bank is not None:
            b0, sz = pe_rows if pe_rows is not None else (0, 128)
            groups = set(range(b0 // 32, (b0 + sz - 1) // 32 + 1))
            last = self.pe_last.get(pe_bank)
            if last is not None and self.serial["pe"] - last[0] < 24 and not (groups & last[1]):
                c = self._signal_upto("pe", last[0])
                if self.seen["pe"].get("pe", 0) < c:
                    self.seen["pe"]["pe"] = c
                    self.ops["pe"].append({"kind": "w", "sem": "pe", "val": c})
            self.pe_last[pe_bank] = (self.serial["pe"] + 1, groups)
        self._deps(eng, reads, writes)
        self.serial[eng] += 1
        ser = self.serial[eng]
        self.ops[eng].append({"kind": "c", "fn": fn, "sig": None})
        for b in reads:
            b.r[eng] = ser
        for b in writes:
            b.w = (eng, ser)
            b.r = {}

    def dma(self, q, fn, reads=(), writes=(), group=None):
        grp = group or q
        k = self.dnext[grp]
        self.dnext[grp] = (k + 1) % self.nslots
        name = "d:%s%d" % (grp, k)
        if self.dcount[name] > 0:
            self._wait(q, name, self.dcount[name])
        self._deps(q, reads, writes)
        self.dcount[name] += 16
        c = self.dcount[name]
        if q in self.serial:
            pass
        self.ops[q].append({"kind": "d", "fn": fn, "sig": (name, 16)})
        for b in reads:
            b.r[name] = c
        for b in writes:
            b.w = (name, c)
            b.r = {}

    def barrier(self):
        engs = ("pe", "act", "dve", "pool", "sp")
        for me in engs:
            for e in self.CE:
                if self.serial[e] > 0 and not (e == me):
                    self._wait_last(me, e)
            for name, c in self.dcount.items():
                if c > 0 and not name.startswith("d:cast"):
                    self._wait(me, name, c)

    def _wait_last(self, me, e):
        c = self._signal_upto(e, self.serial[e])
        if self.seen[me].get(e, 0) >= c:
            return
        self.seen[me][e] = c
        self.ops[me].append({"kind": "w", "sem": e, "val": c})

    def final_wait(self):
        for name, c in self.dcount.items():
            if c > 0:
                self._wait("sp", name, c)
        for e in self.CE:
            if self.serial[e] > 0:
                self._wait_last("sp", e)

    def replay(self, eng_name, eng, sems):
        for o in self.ops[eng_name]:
            if o["kind"] == "w":
                eng.wait_ge(sems[o["sem"]], o["val"])
            else:
                ins = o["fn"](eng)
                if o["sig"] is not None:
                    ins.then_inc(sems[o["sig"][0]], o["sig"][1])


def _tables(Tc, pad_first=0):
    H, DK = 8, 64
    g = 1.0 - np.exp2(-5.0 - np.arange(H, dtype=np.float64))
    logg = np.log(g)
    freqs = 10000.0 ** (-np.arange(32, dtype=np.float64) / 32.0)
    pos = np.arange(Tc, dtype=np.float64)
    ang = pos[:, None] * freqs[None, :]
    cos, sin = np.cos(ang), np.sin(ang)
    CQ = np.zeros((128, 4, 64)); SQ = np.zeros((128, 4, 64)); CK = np.zeros((128, 4, 64)); SK = np.zeros((128, 4, 64))
    for par in range(2):
        for d in range(64):
            f = d % 32
            sgn = -1.0 if d < 32 else 1.0
            for m in range(4):
                h = 2 * m + par
                dq = np.exp(logg[h] * pos) * DK ** -0.5
                dk = np.exp(-logg[h] * pos)
                CQ[par * 64 + d, m, :Tc] = cos[:, f] * dq
                SQ[par * 64 + d, m, :Tc] = sgn * sin[:, f] * dq
                CK[par * 64 + d, m, :Tc] = cos[:, f] * dk
                SK[par * 64 + d, m, :Tc] = sgn * sin[:, f] * dk
    CK0, SK0 = CK.copy(), SK.copy()
    CK0[:, :, :pad_first] = 0.0
    SK0[:, :, :pad_first] = 0.0
    CC = np.zeros((64, 8, 64)); SS = np.zeros((64, 8, 64))
    ang2 = (pos[:, None] - Tc) * freqs[None, :]
    cos2, sin2 = np.cos(ang2), np.sin(ang2)
    for h in range(8):
        dec = np.exp(logg[h] * (Tc - pos))
        for d in range(64):
            f = d % 32
            sgn = -1.0 if d < 32 else 1.0
            CC[:Tc, h, d] = cos2[:, f] * dec
            SS[:Tc, h, d] = sgn * sin2[:, f] * dec
    CC0, SS0 = CC.copy(), SS.copy()
    CC0[:pad_first] = 0.0
    SS0[:pad_first] = 0.0
    CM = np.zeros((64, 8, 64))
    for j in range(Tc):
        CM[j, :, j:Tc] = 1.0
    GT = np.zeros((128, 4))
    for par in range(2):
        for m in range(4):
            GT[par * 64:(par + 1) * 64, m] = np.exp(logg[2 * m + par] * Tc)
    shift = -Tc * freqs
    c, s = np.cos(shift), np.sin(shift)
    R = np.zeros((64, 64))
    for f in range(32):
        R[f, f] = c[f]; R[f, f + 32] = -s[f]
        R[32 + f, 32 + f] = c[f]; R[32 + f, f] = s[f]
    ROT = np.zeros((128, 128))
    ROT[:64, :64] = R.T
    ROT[64:, 64:] = R.T
    import ml_dtypes
    RG = np.zeros((128, 4, 128))
    for m in range(4):
        RG[:, m, :] = ROT * GT[:, m][:, None]
    r32 = RG.reshape(128, 512).astype(np.float32)
    rhi = r32.astype(ml_dtypes.bfloat16).astype(np.float32)
    rlo = (r32 - rhi).astype(ml_dtypes.bfloat16).astype(np.float32)
    ROT = np.concatenate([rhi, rlo], axis=1)
    fm = np.concatenate([CQ, SQ, CK, SK, CK0, SK0], axis=1).reshape(128, 24 * 64)
    tm = np.zeros((128, 5 * 512))
    tm[:64] = np.concatenate([CC, SS, CC0, SS0, CM], axis=1).reshape(64, 5 * 512)
    return (fm.astype(np.float32), tm.astype(np.float32), GT.astype(np.float32), ROT.astype(np.float32))


def _attn_masks():
    M0 = np.full((128, 256), NEG); M1 = np.full((128, 256), NEG); MG = np.full((128, 256), NEG)
    M0[:64, 48:64] = 0.0
    M0[64:, 48:128] = 0.0
    M1[:64, 48:192] = 0.0
    M1[64:, 64:256] = 0.0
    MG[:64, 0:192] = 0.0
    MG[64:, 64:256] = 0.0
    return np.concatenate([M0, M1, MG], axis=1).astype(np.float32)


def _perm_matrix():
    P = np.zeros((128, 128), np.float32)
    for m in range(128):
        k = m + 32 if (m % 64) < 32 else m - 32
        P[k, m] = 1.0
    return P


def _perm_cols():
    qa0, ka0, va0, qr0, kr0, vr0, gr0 = 0, 512, 640, 768, 1280, 1792, 2304
    cols = []
    for m in range(4):
        cols += list(range(qa0 + m * 64, qa0 + (m + 1) * 64)) + list(range(qa0 + (4 + m) * 64, qa0 + (5 + m) * 64))
    cols += list(range(ka0, ka0 + 128))
    for base in (qr0, kr0):
        for m in range(4):
            nat = list(range(base + m * 128, base + (m + 1) * 128))
            per = []
            for hh in range(2):
                b = base + m * 128 + hh * 64
                per += list(range(b + 32, b + 64)) + list(range(b, b + 32))
            cols += nat + per
    cols += list(range(gr0, gr0 + 512))
    assert len(cols) == NEXT_F * 128
    tok = list(range(kr0, kr0 + 512)) + list(range(vr0, vr0 + 512)) + list(range(ka0, ka0 + 128)) + list(range(va0, va0 + 128))
    return np.array(cols), np.array(tok)


def build(cfg):
    NPS, S, NSS, DEPTH = cfg["NPS"], cfg["S"], cfg["NSS"], cfg["DEPTH"]
    TP = S + 64
    TSM = NSS * 32
    TMAX = max(TP, TSM)
    NSUB = (TP + 127) // 128
    nc = bass.Bass("TRN2", target_bir_lowering=False)
    tr = TR()

    def din(name, shape, dt=F32):
        return nc.dram_tensor(name, list(shape), dt, kind="ExternalInput").ap()

    def dout(name, shape):
        return nc.dram_tensor(name, list(shape), F32, kind="ExternalOutput").ap()

    def dscr(name, shape, dt=BF16):
        return nc.dram_tensor(name, list(shape), dt, kind="Internal").ap()

    xp_d = din("xp", [NPS, S, D])
    xs_d = din("xs", [TSM, D])
    ck_d = din("ck", [DEPTH, NSS, 128, 128])
    cv_d = din("cv", [DEPTH, NSS, 128, 128])
    st_d = din("st", [DEPTH, NSS, 8, 64, 64])
    meta_d = din("meta", [16, D])
    w1_d = din("w1", [DEPTH * 2, NJ, 128, KC * 256])
    w2_d = din("w2", [DEPTH * 2, KC, 128, NJ * 128])
    wf_d = din("wf", [DEPTH, NEXT_F, 128, KC * 128])
    wt_d = din("wt", [DEPTH, 128, KC * NTOK])
    wo_d = din("wo", [DEPTH, KC, 128, KC * 128])
    gains_d = din("gains", [128, (DEPTH * 6 + 1) * KC])
    sinks_d = din("sinks", [1, DEPTH * 8])
    tabfm_d = din("tabfm", [2, 128, 24 * 64])
    tabtm_d = din("tabtm", [2, 128, 5 * 512])
    gt_d = din("gt", [2, 128, 4])
    rot_d = din("rot", [2, 128, 1024])
    masks_d = din("masks", [128, 768])
    ident_d = din("ident", [128, 128])
    perm_d = din("perm", [128, 128])

    yp_d = dout("yp", [NPS, S, D])
    ys_d = dout("ys", [TSM, D])
    pk_d = dout("pk", [DEPTH, NPS, 128, 128])
    pv_d = dout("pv", [DEPTH, NPS, 128, 128])
    pr_d = dout("pr", [DEPTH, NPS, 8, 64, 64])
    sk_d = dout("sk", [DEPTH, NSS, 128, 128])
    sv_d = dout("sv", [DEPTH, NSS, 128, 128])
    sr_d = dout("sr", [DEPTH, NSS, 8, 64, 64])

    w1_b = dscr("w1b", [DEPTH * 2, NJ, 128, KC * 256])
    w2_b = dscr("w2b", [DEPTH * 2, KC, 128, NJ * 128])
    wf_b = dscr("wfb", [DEPTH, NEXT_F, 128, KC * 128])
    wt_b = dscr("wtb", [DEPTH, 128, KC * NTOK])
    wo_b = dscr("wob", [DEPTH, KC, 128, KC * 128])
    B_w1 = [[Buf("w1b") for _ in range(NJ)] for _ in range(DEPTH * 2)]
    B_w2 = [[Buf("w2b") for _ in range(KC)] for _ in range(DEPTH * 2)]
    B_wf = [[Buf("wfb") for _ in range(NEXT_F)] for _ in range(DEPTH)]
    B_wt = [Buf("wtb") for _ in range(DEPTH)]
    B_wo = [[Buf("wob") for _ in range(KC)] for _ in range(DEPTH)]

    NTF = 704
    A_BF = 34400
    A_F = 9160
    ctx = []

    def sb(name, shape, dt):
        g = nc.sbuf_tensor(name, list(shape), dt)
        t = g.__enter__()
        ctx.append(g)
        return t

    X = sb("X", [128, KC, TMAX], F32)
    hB = sb("hB", [128, KC, NTF], BF16)
    abf = sb("abf", [128, A_BF], BF16)
    af = sb("af", [128, A_F], F32)
    identF = sb("identF", [128, 128], F32)
    identB = sb("identB", [128, 128], BF16)
    onesB = sb("onesB", [128, 128], BF16)
    rgB = sb("rgB", [128, 1024], BF16)
    bmask = sb("bmask", [128, 128], BF16)
    permB = sb("permB", [128, 128], BF16)
    tabfm = sb("tabfm_s", [128, 24 * 64], F32)
    tabtm = sb("tabtm_s", [128, 5 * 512], F32)
    gtT = sb("gtT", [128, 4], F32)
    masksB = sb("masksB", [128, 768], BF16)
    gains = sb("gains_s", [128, (DEPTH * 6 + 1) * KC], F32)
    sinksT = sb("sinksT", [128, DEPTH * 8], F32)
    gp = nc.psum_tensor("ps", [128, 4096], F32)
    ps = gp.__enter__()
    ctx.append(gp)
    PB = [Buf("bank%d" % i, excl=True) for i in range(8)]

    def bank(i, n=512, off=0):
        return ps[:, i * 512 + off:i * 512 + off + n]

    B_const = Buf("const")
    B_tabs = Buf("tabs")
    B_h = [Buf("h%d" % i) for i in range(4)]
    XB = [Buf("x%d" % i) for i in range((TMAX + 63) // 64)]

    def xb(t0, n):
        return XB[t0 // 64:(t0 + n + 63) // 64]

    class Arena:
        def __init__(self, t, size):
            self.t, self.size, self.off = t, size, 0

        def reset(self):
            self.off = 0

        def take(self, n):
            assert self.off + n <= self.size, (self.off, n, self.size)
            a = self.t[:, self.off:self.off + n]
            self.off += n
            return a

    ABF = Arena(abf, A_BF)
    AFF = Arena(af, A_F)

    def _bank_of(writes):
        return int(writes[0].name[4:])

    def MM(out, lhsT, rhs, start, stop, reads, writes, rows=None):
        tr.op("pe", lambda e: e.matmul(out, lhsT=lhsT, rhs=rhs, start=start, stop=stop), reads, writes,
              pe_bank=_bank_of(writes), pe_rows=rows)

    def TP_(out, in_, ident, reads, writes, rows=None):
        tr.op("pe", lambda e: e.transpose(out, in_, ident), reads, writes, pe_bank=_bank_of(writes), pe_rows=rows)

    def ACT(out, in_, func, reads, writes, bias=None, scale=None, accum_out=None):
        kw = {}
        if bias is not None:
            kw["bias"] = bias
        if scale is not None:
            kw["scale"] = scale
        if accum_out is not None:
            kw["accum_out"] = accum_out
        tr.op("act", lambda e: e.activation(out=out, in_=in_, func=func, **kw), reads, writes)

    def TT(out, in0, in1, op, reads, writes, eng="dve"):
        tr.op(eng, lambda e: e.tensor_tensor(out=out, in0=in0, in1=in1, op=op), reads, writes)

    def TS(out, in0, s1, op0, reads, writes, s2=None, op1=None, eng="dve"):
        if op1 is None:
            tr.op(eng, lambda e: e.tensor_scalar(out=out, in0=in0, scalar1=s1, scalar2=None, op0=op0), reads, writes)
        else:
            tr.op(eng, lambda e: e.tensor_scalar(out=out, in0=in0, scalar1=s1, scalar2=s2, op0=op0, op1=op1), reads, writes)

    def STT(out, in0, scalar, in1, op0, op1, reads, writes, eng="dve"):
        tr.op(eng, lambda e: e.scalar_tensor_tensor(out=out, in0=in0, scalar=scalar, in1=in1, op0=op0, op1=op1), reads, writes)

    def CP(out, in_, reads, writes, eng="dve"):
        tr.op(eng, lambda e: e.tensor_copy(out=out, in_=in_), reads, writes)

    def RECIP(out, in_, reads, writes):
        tr.op("dve", lambda e: e.reciprocal(out=out, in_=in_), reads, writes)

    def MEMSET(ap, val, writes, eng="dve"):
        tr.op(eng, lambda e: e.memset(ap, val), (), writes)

    def DMA(q, out, in_, reads, writes, group=None):
        tr.dma(q, lambda e: e.dma_start(out=out, in_=in_), reads, writes, group=group)

    def gcol(which, l, kc):
        i = (l * 6 + which) * KC + kc if which < 6 else DEPTH * 6 * KC + kc
        return gains[:, i:i + 1]

    DMA("sp", identF[:], ident_d[:, :], (), (B_const,))
    DMA("sp", gains[:], gains_d[:, :], (), (B_const,))
    DMA("sp", sinksT[:], sinks_d[0:1, :].partition_broadcast(128), (), (B_const,))
    stage = X[:, 0:3, 0:256]
    DMA("sp", stage, masks_d[:, :].rearrange("p (a b) -> p a b", a=3), (), [B_const] + xb(0, 256))
    CP(identB[:], identF[:], (B_const,), (B_const,))
    CP(masksB[:].rearrange("p (a b) -> p a b", a=3), stage, [B_const] + xb(0, 256), (B_const,))
    DMA("sp", X[:, 4, 0:128], perm_d[:, :], (), [B_const] + xb(0, 128))
    CP(permB[:], X[:, 4, 0:128], [B_const] + xb(0, 128), (B_const,))
    epsT = sb("epsT", [128, 1], F32)
    MEMSET(epsT[:], EPS, (B_const,), eng="pool")
    MEMSET(onesB[:], 1.0, (B_const,), eng="pool")
    MEMSET(bmask[:], 0.0, (B_const,), eng="pool")
    MEMSET(bmask[0:64, 0:64], 1.0, (B_const,), eng="pool")
    MEMSET(bmask[64:128, 64:128], 1.0, (B_const,), eng="pool")

    def load_tabs(idx):
        DMA("sp", tabfm[:], tabfm_d[idx], (), (B_tabs,))
        DMA("sp", tabtm[:], tabtm_d[idx], (), (B_tabs,))
        DMA("sp", gtT[:], gt_d[idx], (), (B_tabs,))
        stg_ = X[:, 0:4, 0:256]
        DMA("sp", stg_, rot_d[idx].rearrange("p (a b) -> p a b", a=4), (), [B_tabs] + xb(0, 256))
        CP(rgB[:].rearrange("p (a b) -> p a b", a=4), stg_, [B_tabs] + xb(0, 256), (B_tabs,))

    for l in range(DEPTH):
        for f in range(2):
            lf = l * 2 + f
            if f == 1:
                for c in range(NEXT_F):
                    DMA("pool", wf_b[l, c], wf_d[l, c], (), (B_wf[l][c],), group="cast")
                DMA("pool", wt_b[l], wt_d[l], (), (B_wt[l],), group="cast")
                for c in range(KC):
                    DMA("pool", wo_b[l, c], wo_d[l, c], (), (B_wo[l][c],), group="cast")
            for j in range(NJ):
                DMA("pool", w1_b[lf, j], w1_d[lf, j], (), (B_w1[lf][j],), group="cast")
            for c in range(KC):
                DMA("pool", w2_b[lf, c], w2_d[lf, c], (), (B_w2[lf][c],), group="cast")

    def rstd_from_sq(sqv, n, ssbank, rt, rstd, rd, wr_bufs, nfeat=1024.0, rt_bufs=None):
        rt_bufs = wr_bufs if rt_bufs is None else rt_bufs
        for kc in range(KC):
            MM(bank(ssbank, n), onesB[:], sqv[:, kc, :], kc == 0, kc == KC - 1, rd + [B_const], [PB[ssbank]])
        ACT(rt, bank(ssbank, n), AF.Sqrt, [PB[ssbank], B_const], rt_bufs, bias=epsT[:, 0:1], scale=1.0 / nfeat)
        RECIP(rstd, rt, rt_bufs, wr_bufs)

    FB = ([Buf("nP0"), Buf("nP1")], [Buf("rt0"), Buf("rt1")], [Buf("w1r") for _ in range(3)],
          [Buf("w2r") for _ in range(2)], [Buf("a0"), Buf("a1")], [Buf("Y0"), Buf("Y1")],
          [Buf("sg0"), Buf("sg1")], [Buf("n0"), Buf("n1")], [Buf("t0"), Buf("t1")])

    def ffn(l, f, T, do_barrier=True):
        lf = l * 2 + f
        ABF.reset(); AFF.reset()
        w1r = [ABF.take(KC * 256).rearrange("p (k c) -> p k c", k=KC) for _ in range(3)]
        w2r = [ABF.take(NJ * 128).rearrange("p (j c) -> p j c", j=NJ) for _ in range(2)]
        aT = ABF.take(NJ * NTF).rearrange("p (j n) -> p j n", j=NJ)
        Y = AFF.take(KC * NTF).rearrange("p (k n) -> p k n", k=KC)
        sg = [AFF.take(352) for _ in range(2)]
        rt = [AFF.take(352) for _ in range(2)]
        rs = [AFF.take(352) for _ in range(2)]
        tmp = [AFF.take(352) for _ in range(2)]
        rsP = [AFF.take(352) for _ in range(2)]
        pending = []
        B_nP, B_rt, B_w1r, B_w2r, B_a, B_Y, B_sg, B_n, B_t = FB
        sts = []
        t = 0
        while t < T:
            n = min(NTF, T - t)
            tiles = []
            o = 0
            while o < n:
                m = min(352, n - o)
                tiles.append((o, m))
                o += m
            sts.append((t, tiles))
            t += n
        cA = 0
        cB = 0
        ring1 = 0
        ring2 = 0
        for (t0, tiles) in sts:
            for ti, (o, n) in enumerate(tiles):
                xs_ = xb(t0 + o, n)
                ACT(hB[:, :, o:o + n], X[:, :, t0 + o:t0 + o + n], AF.Square, xs_, [B_h[ti]])
                rstd_from_sq(hB[:, :, o:o + n], n, 4 + ti, rt[ti][:, :n], rs[ti][:, :n], [B_h[ti]], [B_n[ti]],
                             rt_bufs=[B_rt[ti]])
                for kc in range(KC):
                    STT(hB[:, kc, o:o + n], X[:, kc, t0 + o:t0 + o + n], gcol(0 if f == 0 else 4, l, kc), rs[ti][:, :n],
                        ALU.mult, ALU.mult, xs_ + [B_n[ti], B_const], [B_h[ti]])
            for j in range(NJ):
                s = ring1 % 3
                ring1 += 1
                if "W" not in cfg.get("stages", "") or ring1 <= 3:
                    DMA("sp", w1r[s], w1_b[lf, j].rearrange("p (k c) -> p k c", k=KC), [B_w1[lf][j]], [B_w1r[s]])
                for ti, (o, n) in enumerate(tiles):
                    gb = (2 * cA) % 4
                    ub = (2 * cA + 1) % 4
                    cA += 1
                    for kc in range(KC):
                        MM(bank(gb, n), w1r[s][:, kc, 0:128], hB[:, kc, o:o + n], kc == 0, kc == KC - 1,
                           [B_w1r[s], B_h[ti]], [PB[gb]])
                    for kc in range(KC):
                        MM(bank(ub, n), w1r[s][:, kc, 128:256], hB[:, kc, o:o + n], kc == 0, kc == KC - 1,
                           [B_w1r[s], B_h[ti]], [PB[ub]])
                    ACT(sg[ti][:, :n], bank(gb, n), AF.Silu, [PB[gb]], [B_sg[ti]])
                    TT(aT[:, j, o:o + n], sg[ti][:, :n], bank(ub, n), ALU.mult, [B_sg[ti], PB[ub]], [B_a[ti]])
                    if pending:
                        pending.pop(0)()
            while pending:
                pending.pop(0)()
            for oc in range(KC):
                s = ring2 % 2
                ring2 += 1
                if "W" not in cfg.get("stages", "") or ring2 <= 2:
                    DMA("sp", w2r[s], w2_b[lf, oc].rearrange("p (j c) -> p j c", j=NJ), [B_w2[lf][oc]], [B_w2r[s]])
                for ti, (o, n) in enumerate(tiles):
                    yb = 4 + (cB % 2)
                    cB += 1
                    for j in range(NJ):
                        MM(bank(yb, n), w2r[s][:, j, :], aT[:, j, o:o + n], j == 0, j == NJ - 1,
                           [B_w2r[s], B_a[ti]], [PB[yb]])
                    ACT(Y[:, oc, o:o + n], bank(yb, n), AF.Copy, [PB[yb]], [B_Y[ti]])
                    ACT(hB[:, oc, o:o + n], bank(yb, n), AF.Square, [PB[yb]], [B_h[ti]])
            for ti, (o, n) in enumerate(tiles):
                rstd_from_sq(hB[:, :, o:o + n], n, 6 + ti, rt[ti][:, :n], rsP[ti][:, :n], [B_h[ti]], [B_nP[ti]],
                             rt_bufs=[B_rt[ti]])
            for ti, (o, n) in enumerate(tiles):
                xs_ = xb(t0 + o, n)
                for oc in range(KC):
                    def item(ti=ti, o=o, n=n, oc=oc, xs_=xs_, t0=t0):
                        STT(tmp[ti][:, :n], Y[:, oc, o:o + n], gcol(1 if f == 0 else 5, l, oc), rsP[ti][:, :n],
                            ALU.mult, ALU.mult, [B_Y[ti], B_nP[ti], B_const], [B_t[ti]])
                        STT(X[:, oc, t0 + o:t0 + o + n], tmp[ti][:, :n], 0.5, X[:, oc, t0 + o:t0 + o + n],
                            ALU.mult, ALU.add, [B_t[ti]] + xs_, xs_)
                    pending.append(item)
        while pending:
            pending.pop(0)()
        if do_barrier:
            tr.barrier()

    def mixer(l, T, Tc, prompt, seq):
        ABF.reset(); AFF.reset()
        NT = 256 if prompt else 128
        wtok = ABF.take(KC * NTOK).rearrange("p (k c) -> p k c", k=KC)
        wfr = [ABF.take(KC * 128).rearrange("p (k c) -> p k c", k=KC) for _ in range(3)]
        qaT = ABF.take(4 * NT).rearrange("p (m n) -> p m n", m=4)
        kaT = ABF.take(T)
        Vtok = ABF.take(NSUB * 128).rearrange("p (s c) -> p s c", c=128)
        qtT = ABF.take(4 * NT).rearrange("p (m n) -> p m n", m=4)
        ktT = ABF.take(4 * NT).rearrange("p (m n) -> p m n", m=4)
        gsT = ABF.take(4 * NT).rearrange("p (m n) -> p m n", m=4)
        khat = [ABF.take(512).rearrange("p (h d) -> p h d", h=8) for _ in range(2)]
        vrb = [ABF.take(512).rearrange("p (h d) -> p h d", h=8) for _ in range(2)]
        mixT = ABF.take(KC * NT).rearrange("p (k n) -> p k n", k=KC)
        E = ABF.take(8 * 256).rearrange("p (h s) -> p h s", h=8)
        PT = ABF.take(16 * 128).rearrange("p (s q) -> p s q", s=16)
        On = ABF.take(512)
        STb = [ABF.take(512).rearrange("p (h i) -> p h i", h=8) for _ in range(2)]
        Sbf = ABF.take(512).rearrange("p (m e) -> p m e", m=4)
        onb = [ABF.take(512) for _ in range(2)]
        Slo = ABF.take(512).rearrange("p (m e) -> p m e", m=4)
        qb = ABF.take(NT)
        cKT = ABF.take(128)
        Vc = ABF.take(128)
        Vn = ABF.take(128)
        Y = AFF.take(KC * NT).rearrange("p (k n) -> p k n", k=KC)
        Sst = AFF.take(512).rearrange("p (m e) -> p m e", m=4)
        Tm = AFF.take(512).rearrange("p (m e) -> p m e", m=4)
        kvf = [AFF.take(256) for _ in range(2)]
        sqo = AFF.take(512)
        rA = AFF.take(512).rearrange("p (h d) -> p h d", h=8)
        rB = AFF.take(512).rearrange("p (h d) -> p h d", h=8)
        t1 = AFF.take(NT)
        t2 = AFF.take(NT)
        cin = [AFF.take(128) for _ in range(2)]
        rt = AFF.take(NT)
        rs = AFF.take(NT)
        tmp = AFF.take(NT)
        stat = AFF.take(64)
        rsP = AFF.take(NT)
        pending = []
        B = {k: Buf(k) for k in ("wtok", "qaT", "kaT", "Vtok", "qtT", "ktT", "gsT", "mixT", "E", "PT", "On", "Sbf",
                                 "cKT", "Vc", "Vn", "Y", "Sst", "Tm", "sqo", "rA", "rB", "t1", "t2", "n", "tmp",
                                 "stat", "h", "rsum", "Th", "Tl", "gn", "rt", "nP", "qb", "stat0", "stat1", "rsum0", "rsum1", "E0", "E1", "PT0", "PT1", "On0", "On1")}
        B_wfr = [Buf("wfr") for _ in range(3)]
        B_khat = [Buf("kh0"), Buf("kh1")]
        B_vr = [Buf("vr0"), Buf("vr1")]
        B_STb = [Buf("stb0"), Buf("stb1")]
        B_onb = [Buf("onb0"), Buf("onb1")]
        B_kvf = [Buf("kvf0"), Buf("kvf1")]
        B_cin = [Buf("cin0"), Buf("cin1")]
        ring = [0]

        def wf_load(src_ap, src_buf):
            s = ring[0] % 3
            ring[0] += 1
            DMA("sp", wfr[s], src_ap.rearrange("p (k c) -> p k c", k=KC), [src_buf], [B_wfr[s]])
            return s

        DMA("sp", wtok, wt_b[l].rearrange("p (k c) -> p k c", k=KC), [B_wt[l]], [B["wtok"]])
        def fm(i):
            return tabfm[:, i * 256:(i + 1) * 256].rearrange("p (m i) -> p m i", m=4)

        def tmv(i, rows):
            return tabtm[0:rows, i * 512:(i + 1) * 512].rearrange("p (h d) -> p h d", h=8)

        sink = sinksT[:, l * 8:(l + 1) * 8]
        if prompt:
            MEMSET(Sst[:], 0.0, [B["Sst"]])
            MEMSET(Sbf[:], 0.0, [B["Sbf"]])
            MEMSET(Slo[:], 0.0, [B["Tl"]])
        cnt = {"f": 0, "tok": 0, "rc": 0}

        tiles = []
        t = 0
        while t < T:
            n = min(NT, T - t)
            tiles.append((t, n))
            t += n

        for (t0, n) in tiles:
            xs_ = xb(t0, n)
            nch = n // Tc
            ACT(hB[:, :, 0:n], X[:, :, t0:t0 + n], AF.Square, xs_, [B["h"]])
            rstd_from_sq(hB[:, :, 0:n], n, 7, rt[:, :n], rs[:, :n], [B["h"]], [B["n"]], rt_bufs=[B["rt"]])
            for kc in range(KC):
                STT(hB[:, kc, 0:n], X[:, kc, t0:t0 + n], gcol(2, l, kc), rs[:, :n], ALU.mult, ALU.mult,
                    xs_ + [B["n"], B_const], [B["h"]])

            def proj(c):
                s = wf_load(wf_b[l, c], B_wf[l][c])
                k = cnt["f"] % 8
                cnt["f"] += 1
                b, off = k // 2, (k % 2) * 256
                for kc in range(KC):
                    MM(bank(b, n, off), wfr[s][:, kc, :], hB[:, kc, 0:n], kc == 0, kc == KC - 1,
                       [B_wfr[s], B["h"]], [PB[b]])
                if pending:
                    pending.pop(0)()
                return b, off

            stg = cfg.get("stages", "xfmy") + ("paro" if not any(c in cfg.get("stages", "xfmy") for c in "paro") else "")
            if "p" in stg and not any(c in stg for c in "123"):
                stg = stg + "123"
            for m in (range(4) if "1" in stg else []):
                b, off = proj(m)
                ACT(qaT[:, m, 0:n], bank(b, n, off), AF.Copy, [PB[b]], [B["qaT"]])
            if "1" in stg:
                b, off = proj(4)
                ACT(kaT[:, t0:t0 + n], bank(b, n, off), AF.Copy, [PB[b]], [B["kaT"]])
            for (cbase, dst, dbuf, ci, si) in (((5, qtT, "qtT", 0, 1), (13, ktT, "ktT", 2, 3)) if "2" in stg else []):
                for m in range(4):
                    b1, o1 = proj(cbase + 2 * m)
                    ACT(qb[:, 0:n], bank(b1, n, o1), AF.Copy, [PB[b1]], [B["qb"]])
                    k2 = cnt["f"] % 8
                    cnt["f"] += 1
                    b2, o2 = k2 // 2, (k2 % 2) * 256
                    MM(bank(b2, n, o2), permB[:, :], qb[:, 0:n], True, True, [B["qb"], B_const], [PB[b2]])
                    v3 = lambda a: a.rearrange("p (c i) -> p c i", i=Tc)
                    TT(v3(t1[:, :n]), v3(bank(b1, n, o1)), fm(ci)[:, m, 0:Tc].unsqueeze(1).to_broadcast([128, nch, Tc]),
                       ALU.mult, [PB[b1], B_tabs], [B["t1"]])
                    TT(v3(t2[:, :n]), v3(bank(b2, n, o2)), fm(si)[:, m, 0:Tc].unsqueeze(1).to_broadcast([128, nch, Tc]),
                       ALU.mult, [PB[b2], B_tabs], [B["t2"]])
                    TT(dst[:, m, 0:n], t1[:, :n], t2[:, :n], ALU.add, [B["t1"], B["t2"]], [B[dbuf]])
                if dbuf == "ktT" and prompt and t0 == 0:
                    MEMSET(ktT[:, :, 0:48], 0.0, [B["ktT"]])
            for m in (range(4) if "3" in stg else []):
                b, off = proj(21 + m)
                ACT(gsT[:, m, 0:n], bank(b, n, off), AF.Silu, [PB[b]], [B["gsT"]])

            def r_front(ch):
                tcs = t0 + ch * Tc
                o_in = ch * Tc
                slot = cnt["rc"] % 2
                cnt["rc"] += 1
                first = prompt and tcs == 0
                for gi in range(2):
                    for kc in range(KC):
                        MM(ps[0:Tc, (4 + gi) * 512:(5 + gi) * 512], hB[:, kc, o_in:o_in + Tc],
                           wtok[:, kc, gi * 512:(gi + 1) * 512], kc == 0, kc == KC - 1,
                           [B["h"], B["wtok"]], [PB[4 + gi]])
                kps = ps[0:Tc, 4 * 512:5 * 512].rearrange("p (h d) -> p h d", h=8)
                yield
                TT(rA[0:Tc], kps, tmv(0, Tc), ALU.mult, [PB[4], B_tabs], [B["rA"]])
                TT(rB[0:Tc, :, 0:32], kps[:, :, 32:64], tmv(1, Tc)[:, :, 0:32], ALU.mult, [PB[4], B_tabs], [B["rB"]])
                TT(rB[0:Tc, :, 32:64], kps[:, :, 0:32], tmv(1, Tc)[:, :, 32:64], ALU.mult, [PB[4], B_tabs], [B["rB"]])
                TT(khat[slot][0:Tc], rA[0:Tc], rB[0:Tc], ALU.add, [B["rA"], B["rB"]], [B_khat[slot]])
                if first:
                    MEMSET(khat[slot][0:48], 0.0, [B_khat[slot]])
                ACT(vrb[slot][0:Tc], ps[0:Tc, 5 * 512:6 * 512].rearrange("p (h d) -> p h d", h=8), AF.Copy,
                    [PB[5]], [B_vr[slot]])
                yield
                for h in range(8):
                    m, par = h // 2, h % 2
                    MM(ps[0:Tc, par * 512 + m * 64:par * 512 + m * 64 + Tc],
                       ktT[par * 64:(par + 1) * 64, m, o_in:o_in + Tc],
                       qtT[par * 64:(par + 1) * 64, m, o_in:o_in + Tc], True, True, [B["ktT"], B["qtT"]], [PB[par]],
                       rows=(par * 64, 64))
                yield
                for par in range(2):
                    TT(STb[slot][0:Tc, par:8:2, 0:Tc],
                       ps[0:Tc, par * 512:par * 512 + 256].rearrange("p (h i) -> p h i", h=4)[:, :, 0:Tc],
                       tmv(4, Tc)[:, 0:4, 0:Tc], ALU.mult, [PB[par], B_tabs], [B_STb[slot]])
                yield
                return slot

            def r_back(ch, slot):
                tcs = t0 + ch * Tc
                o_in = ch * Tc
                if not prompt:
                    sq_i = tcs // 32
                    MEMSET(Sst[:], 0.0, [B["Sst"]])
                    for par in range(2):
                        DMA("sp", Sst[par * 64:(par + 1) * 64, :, par * 64:(par + 1) * 64],
                            st_d[l, sq_i].rearrange("(m r) d e -> r d m e", r=2)[par], [], [B["Sst"]])
                    CP(Sbf[:], Sst[:], [B["Sst"]], [B["Sbf"]])
                    TT(Slo[:], Sst[:], Sbf[:], ALU.subtract, [B["Sst"], B["Sbf"]], [B["Tl"]])
                for m in range(4):
                    MM(ps[:, 7 * 512 + m * 128:7 * 512 + (m + 1) * 128],
                       khat[slot][0:Tc, 2 * m:2 * m + 2, :].rearrange("p h d -> p (h d)"),
                       vrb[slot][0:Tc, 2 * m:2 * m + 2, :].rearrange("p h d -> p (h d)"), m == 0, False,
                       [B_khat[slot], B_vr[slot]], [PB[7]], rows=(0, Tc))
                for m in range(4):
                    reg = ps[:, 7 * 512 + m * 128:7 * 512 + (m + 1) * 128]
                    MM(reg, rgB[:, m * 128:(m + 1) * 128], Sbf[:, m, :], False, False, [B["Sbf"], B_tabs], [PB[7]])
                    MM(reg, rgB[:, m * 128:(m + 1) * 128], Slo[:, m, :], False, False, [B["Tl"], B_tabs], [PB[7]])
                    MM(reg, rgB[:, 512 + m * 128:512 + (m + 1) * 128], Sbf[:, m, :], False, m == 3,
                       [B["Sbf"], B_tabs], [PB[7]])
                for h in range(8):
                    MM(ps[0:Tc, 2 * 512 + h * 64:2 * 512 + (h + 1) * 64], STb[slot][0:Tc, h, 0:Tc],
                       vrb[slot][0:Tc, h, :], h == 0, False, [B_STb[slot], B_vr[slot]], [PB[2]], rows=(0, Tc))
                for m in range(4):
                    MM(ps[0:Tc, 2 * 512 + m * 128:2 * 512 + (m + 1) * 128], qtT[:, m, o_in:o_in + Tc],
                       Sbf[:, m, :], False, m == 3, [B["qtT"], B["Sbf"]], [PB[2]])
                yield
                TT(Sst[:], ps[:, 7 * 512:8 * 512].rearrange("p (m e) -> p m e", m=4),
                   bmask[:].unsqueeze(1).to_broadcast([128, 4, 128]), ALU.mult, [PB[7], B_const], [B["Sst"]])
                CP(Sbf[:], Sst[:], [B["Sst"]], [B["Sbf"]])
                TT(Slo[:], Sst[:], Sbf[:], ALU.subtract, [B["Sst"], B["Sbf"]], [B["Tl"]])
                yield
                ACT(sqo[0:Tc, :], ps[0:Tc, 1024:1536], AF.Square, [PB[2]], [B["sqo"]])
                gn = stat[0:Tc, 56:64]
                tr.op("dve", lambda e, o=gn, i=sqo[0:Tc, :].rearrange("p (h d) -> p h d", h=8):
                      e.tensor_reduce(out=o, in_=i, axis=AX.X, op=ALU.add), [B["sqo"]], [B["gn"]])
                yield
                ACT(gn, gn, AF.Sqrt, [B["gn"], B_const], [B["gn"]], bias=epsT[0:Tc, 0:1], scale=1.0 / 64.0)
                RECIP(gn, gn, [B["gn"]], [B["gn"]])
                TT(onb[slot][0:Tc, :].rearrange("p (h d) -> p h d", h=8),
                   ps[0:Tc, 1024:1536].rearrange("p (h d) -> p h d", h=8),
                   gn.unsqueeze(2).to_broadcast([Tc, 8, 64]), ALU.mult, [PB[2], B["gn"]], [B_onb[slot]])
                yield
                psb2 = ps[:, 3 * 512:4 * 512].bitcast(BF16).rearrange("p (m q) -> p m q", m=8)
                for m in range(4):
                    TP_(psb2[:, m, 0:Tc], onb[slot][0:Tc, m * 128:(m + 1) * 128], identB[0:Tc, 0:Tc],
                        [B_onb[slot], B_const], [PB[3]], rows=(0, Tc))
                yield
                TT(mixT[:, 4:8, o_in:o_in + Tc], psb2[:, 0:4, 0:Tc], gsT[:, :, o_in:o_in + Tc], ALU.mult,
                   [PB[3], B["gsT"]], [B["mixT"]])
                last = (tcs + Tc >= T) if prompt else True
                if last:
                    dst = pr_d[l, seq] if prompt else sr_d[l, tcs // 32]
                    for par in range(2):
                        DMA("pool", dst.rearrange("(m r) d e -> r d m e", r=2)[par],
                            Sst[par * 64:(par + 1) * 64, :, par * 64:(par + 1) * 64], [B["Sst"]], [])

            def attn_half(hs, bs, hf, ta, nq, o_in, segs, mask, rdK):
                g = hs // 4
                nk = sum(sg_[2] for sg_ in segs)
                hsl = slice(hs, hs + 4)
                Bst, Brs, BE, BPT, BOn = B["stat%d" % hf], B["rsum%d" % hf], B["E%d" % hf], B["PT%d" % hf], B["On%d" % hf]
                for hh in range(4):
                    m = hh
                    b = bs + hh // 2
                    col = (hh % 2) * 256
                    if mask is not None:
                        MM(ps[0:nq, b * 512 + col:b * 512 + col + nk], identB[0:nq, 0:nq], mask[0:nq, 0:nk],
                           True, False, [B_const], [PB[b]], rows=(0, nq))
                    k0 = 0
                    for si_, (kt_ap, v_ap, kn) in enumerate(segs):
                        MM(ps[0:nq, b * 512 + col + k0:b * 512 + col + k0 + kn],
                           qaT[g * 64:(g + 1) * 64, m, o_in:o_in + nq], kt_ap[g * 64:(g + 1) * 64, 0:kn],
                           mask is None, (mask is None) or si_ == len(segs) - 1, [B["qaT"]] + rdK, [PB[b]],
                           rows=(g * 64, 64))
                        k0 += kn
                yield
                Sv = ps[0:nq, bs * 512:bs * 512 + 1024].rearrange("p (h s) -> p h s", h=4)
                mx, mx2, nb, rsum = stat[0:nq, 0:8][:, hsl], stat[0:nq, 8:16][:, hsl], stat[0:nq, 16:24][:, hsl], stat[0:nq, 24:32][:, hsl]
                sk_, esk, rinv = stat[0:nq, 32:40][:, hsl], stat[0:nq, 40:48][:, hsl], stat[0:nq, 48:56][:, hsl]
                tr.op("dve", lambda e, o=mx, i=Sv[:, :, 0:nk]: e.tensor_reduce(out=o, in_=i, axis=AX.X, op=ALU.max),
                      PB[bs:bs + 2], [Bst])
                STT(mx2, mx, 0.125, sink[0:nq, hsl], ALU.mult, ALU.max, [Bst, B_const], [Bst])
                TS(nb, mx2, -1.0, ALU.mult, [Bst], [Bst])
                TT(sk_, sink[0:nq, hsl], mx2, ALU.subtract, [Bst, B_const], [Bst])
                yield
                for hh in range(4):
                    ACT(E[0:nq, hs + hh, 0:nk], Sv[:, hh, 0:nk], AF.Exp, [PB[bs + hh // 2], Bst], [BE, Brs],
                        bias=nb[:, hh:hh + 1], scale=0.125, accum_out=rsum[:, hh:hh + 1])
                ACT(esk, sk_, AF.Exp, [Bst], [Bst])
                yield
                psb = ps[:, (bs + 2) * 512:(bs + 3) * 512].bitcast(BF16).rearrange("p (s q) -> p s q", s=8)
                for hh in range(4):
                    k0 = 0
                    for si_, (kt_ap, v_ap, kn) in enumerate(segs):
                        TP_(psb[0:kn, hh * 2 + si_, 0:nq], E[0:nq, hs + hh, k0:k0 + kn], identB[0:nq, 0:nq],
                            [BE, B_const], [PB[bs + 2]], rows=(0, nq))
                        k0 += kn
                TT(esk, esk, rsum, ALU.add, [Bst, Brs], [Bst])
                RECIP(rinv, esk, [Bst], [Bst])
                yield
                for si_, (kt_ap, v_ap, kn) in enumerate(segs):
                    ACT(PT[0:kn, hs * 2 + si_:hs * 2 + 8:2, 0:nq], psb[0:kn, si_:8:2, 0:nq], AF.Copy,
                        [PB[bs + 2]], [BPT])
                yield
                ob = (bs + 3) * 512
                for hh in range(4):
                    for si_, (kt_ap, v_ap, kn) in enumerate(segs):
                        MM(ps[0:nq, ob + hh * 64:ob + (hh + 1) * 64], PT[0:kn, (hs + hh) * 2 + si_, 0:nq],
                           v_ap[0:kn, g * 64:(g + 1) * 64], si_ == 0, si_ == len(segs) - 1,
                           [BPT] + rdK, [PB[bs + 3]], rows=(0, kn))
                yield
                TT(On[0:nq, hs * 64:(hs + 4) * 64].rearrange("p (h d) -> p h d", h=4),
                   ps[0:nq, ob:ob + 256].rearrange("p (h d) -> p h d", h=4),
                   rinv.unsqueeze(2).to_broadcast([nq, 4, 64]), ALU.mult, [PB[bs + 3], Bst], [BOn])
                yield
                psbT = ps[:, ob + 256:ob + 512].bitcast(BF16).rearrange("p (m q) -> p m q", m=4)
                for mm in range(2):
                    mfeat = hs // 2 + mm
                    TP_(psbT[:, mm, 0:nq], On[0:nq, mfeat * 128:(mfeat + 1) * 128], identB[0:nq, 0:nq],
                        [BOn, B_const], [PB[bs + 3]], rows=(0, nq))
                yield
                ACT(mixT[:, hs // 2:hs // 2 + 2, o_in:o_in + nq], psbT[:, 0:2, 0:nq], AF.Copy,
                    [PB[bs + 3]], [B["mixT"]])
                yield

            def attn_all():
                au_gens = []
                if prompt:
                    aus = [(t0 + o, min(128, n - o)) for o in range(0, n, 128)]
                else:
                    aus = [(t0 + o, 32) for o in range(0, n, 32)]
                for (ta, nq) in aus:
                    o_in = ta - t0
                    sub = ta // 128
                    kslot = cnt["tok"] % 2
                    cnt["tok"] += 1
                    for kc in range(KC):
                        MM(ps[0:nq, 3 * 512:3 * 512 + 256], hB[:, kc, o_in:o_in + nq], wtok[:, kc, 1024:1280],
                           kc == 0, kc == KC - 1, [B["h"], B["wtok"]], [PB[3]])
                    ACT(kvf[kslot][0:nq, :], ps[0:nq, 3 * 512:3 * 512 + 256], AF.Copy, [PB[3]], [B_kvf[kslot]])
                    if prompt:
                        CP(Vtok[0:nq, sub, :], kvf[kslot][0:nq, 128:256], [B_kvf[kslot]], [B["Vtok"]])
                        lo = max(ta, T - 128)
                        hi = ta + nq
                        if hi > lo:
                            r0 = lo - (T - 128)
                            DMA("pool", pk_d[l, seq, r0:r0 + hi - lo, :], kvf[kslot][lo - ta:hi - ta, 0:128],
                                [B_kvf[kslot]], [])
                            DMA("pool", pv_d[l, seq, r0:r0 + hi - lo, :], kvf[kslot][lo - ta:hi - ta, 128:256],
                                [B_kvf[kslot]], [])
                        segs = []
                        if sub > 0:
                            segs.append((kaT[:, ta - 128:ta], Vtok[:, sub - 1, :], 128))
                        segs.append((kaT[:, ta:ta + nq], Vtok[:, sub, :], nq))
                        if sub == 0:
                            mask = masksB[:, 0:256]
                        elif sub == 1:
                            mask = masksB[:, 256:512]
                        else:
                            mask = masksB[:, 512:768]
                        if nq < 128:
                            assert sub >= 2
                            mask = None
                        rdK = [B["kaT"], B["Vtok"]]
                    else:
                        sq_i = ta // 32
                        CP(Vn[0:32, :], kvf[kslot][0:32, 128:256], [B_kvf[kslot]], [B["Vn"]])
                        DMA("sp", cin[0][:, :], ck_d[l, sq_i], [], [B_cin[0]])
                        DMA("sp", cin[1][:, :], cv_d[l, sq_i], [], [B_cin[1]])
                        TP_(bank(3, 128, 256), cin[0][:, :], identF[:], [B_cin[0], B_const], [PB[3]])
                        ACT(cKT[:, :], bank(3, 128, 256), AF.Copy, [PB[3]], [B["cKT"]])
                        CP(Vc[:, :], cin[1][:, :], [B_cin[1]], [B["Vc"]])
                        DMA("pool", sk_d[l, sq_i, 0:96, :], ck_d[l, sq_i, 32:128, :], [], [])
                        DMA("pool", sv_d[l, sq_i, 0:96, :], cv_d[l, sq_i, 32:128, :], [], [])
                        DMA("pool", sk_d[l, sq_i, 96:128, :], kvf[kslot][0:32, 0:128], [B_kvf[kslot]], [])
                        DMA("pool", sv_d[l, sq_i, 96:128, :], kvf[kslot][0:32, 128:256], [B_kvf[kslot]], [])
                        segs = [(cKT[:, :], Vc[:, :], 128), (kaT[:, ta:ta + 32], Vn[:, :], 32)]
                        mask = None
                        rdK = [B["kaT"], B["cKT"], B["Vc"], B["Vn"]]
                    g0 = attn_half(0, 0, 0, ta, nq, o_in, segs, mask, rdK)
                    g1 = attn_half(4, 4, 1, ta, nq, o_in, segs, mask, rdK)
                    if prompt:
                        au_gens.append((g0, g1))
                    else:
                        yield
                        for _ in zip(g0, g1):
                            yield
                if au_gens:
                    SK = 3
                    live = []
                    step = 0
                    nxt = 0
                    while nxt < len(au_gens) or live:
                        if nxt < len(au_gens) and step % SK == 0 and len(live) < 2 + 1:
                            live.append(au_gens[nxt])
                            nxt += 1
                        for pair in list(live):
                            done = False
                            for g in pair:
                                try:
                                    next(g)
                                except StopIteration:
                                    done = True
                            if done:
                                live.remove(pair)
                        step += 1
                        yield
            def ret_pipeline():
                nslot = yield from r_front(0)
                for ch in range(nch):
                    cur = nslot
                    if ch + 1 < nch:
                        nslot = yield from r_front(ch + 1)
                    yield from r_back(ch, cur)

            for _ in attn_all():
                pass
            for _ in ret_pipeline():
                pass

            for oc in (range(KC) if "o" in stg else []):
                s = wf_load(wo_b[l, oc], B_wo[l][oc])
                yb = cnt["f"] % 8
                cnt["f"] += 1
                b, off = yb // 2, (yb % 2) * 256
                for kc in range(KC):
                    MM(bank(b, n, off), wfr[s][:, kc, :], mixT[:, kc, 0:n], kc == 0, kc == KC - 1,
                       [B_wfr[s], B["mixT"]], [PB[b]])
                ACT(Y[:, oc, 0:n], bank(b, n, off), AF.Copy, [PB[b]], [B["Y"]])
                ACT(hB[:, oc, 0:n], bank(b, n, off), AF.Square, [PB[b]], [B["h"]])
            while pending:
                pending.pop(0)()
            rstd_from_sq(hB[:, :, 0:n], n, 7, rt[:, :n], rsP[:, :n], [B["h"]], [B["nP"]], rt_bufs=[B["rt"]])
            for oc in range(KC):
                def item(oc=oc, n=n, t0=t0, xs_=xs_):
                    STT(tmp[:, :n], Y[:, oc, 0:n], gcol(3, l, oc), rsP[:, :n], ALU.mult, ALU.mult,
                        [B["Y"], B["nP"], B_const], [B["tmp"]])
                    TT(X[:, oc, t0:t0 + n], tmp[:, :n], X[:, oc, t0:t0 + n], ALU.add, [B["tmp"]] + xs_, xs_)
                pending.append(item)
        while pending:
            pending.pop(0)()
        tr.barrier()

    def load_x(T, prompt, seq):
        AFF.reset()
        xin = [AFF.take(1024) for _ in range(4)]
        B_xin = [Buf("xin%d" % i) for i in range(4)]
        k = 0
        if prompt:
            MEMSET(X[:, :, 0:48], 0.0, xb(0, 48))
            units = [(48, 16, meta_d[:, :])] + [(64 + i * 128, 128, xp_d[seq, i * 128:(i + 1) * 128, :])
                                               for i in range(S // 128)]
        else:
            units = [(0, T, xs_d[:, :])]
        for (t0, n, src) in units:
            s = k % 4
            DMA("sp", xin[s][0:n, :], src, [], [B_xin[s]])
            for half in range(2):
                b = 2 * (k % 2) + half
                for q in range(4):
                    kc = half * 4 + q
                    TP_(ps[:, b * 512 + q * 128:b * 512 + q * 128 + n], xin[s][0:n, kc * 128:(kc + 1) * 128],
                        identF[0:n, 0:n], [B_xin[s], B_const], [PB[b]], rows=(0, n))
                ACT(X[:, half * 4:half * 4 + 4, t0:t0 + n],
                    ps[:, b * 512:(b + 1) * 512].rearrange("p (q n) -> p q n", q=4)[:, :, 0:n], AF.Copy,
                    [PB[b]], xb(t0, n))
            k += 1
        tr.barrier()

    def store_y(T, prompt, seq):
        AFF.reset()
        yn = [AFF.take(1024).rearrange("p (k n) -> p k n", k=KC) for _ in range(2)]
        yo = [AFF.take(1024) for _ in range(2)]
        rt2 = [AFF.take(128) for _ in range(2)]
        rs2 = [AFF.take(128) for _ in range(2)]
        B_yn = [Buf("yn0"), Buf("yn1")]
        B_yo = [Buf("yo0"), Buf("yo1")]
        B_n2 = [Buf("n0"), Buf("n1")]
        B_hh2 = [Buf("h0"), Buf("h1")]
        if prompt:
            units = [(64 + i * 128, 128, yp_d[seq, i * 128:(i + 1) * 128, :]) for i in range(S // 128)]
        else:
            units = [(0, T, ys_d[:, :])]
        for k, (t0, n, dst) in enumerate(units):
            s = k % 2
            xs_ = xb(t0, n)
            hv = hB[:, :, s * 128:s * 128 + n]
            ACT(hv, X[:, :, t0:t0 + n], AF.Square, xs_, [B_hh2[s]])
            rstd_from_sq(hv, n, 6 + s, rt2[s][:, :n], rs2[s][:, :n], [B_hh2[s]], [B_n2[s]])
            for kc in range(KC):
                STT(yn[s][:, kc, 0:n], X[:, kc, t0:t0 + n], gcol(6, 0, kc), rs2[s][:, :n], ALU.mult, ALU.mult,
                    xs_ + [B_n2[s], B_const], [B_yn[s]])
            for half in range(2):
                b = 2 * (k % 2) + half
                for q in range(4):
                    kc = half * 4 + q
                    TP_(ps[0:n, b * 512 + q * 128:b * 512 + (q + 1) * 128], yn[s][:, kc, 0:n], identF[:, :],
                        [B_yn[s], B_const], [PB[b]])
                ACT(yo[s][0:n, half * 512:(half + 1) * 512], ps[0:n, b * 512:(b + 1) * 512], AF.Copy, [PB[b]], [B_yo[s]])
            DMA("pool", dst, yo[s][0:n, :], [B_yo[s]], [])
        tr.barrier()

    load_tabs(0)
    passes = [("p", i) for i in range(NPS)] + ([("s", 0)] if NSS > 0 else [])
    for (kind, seq) in passes:
        prompt = kind == "p"
        T = TP if prompt else TSM
        Tc = 64 if prompt else 32
        if not prompt:
            tr.barrier()
            load_tabs(1)
        stg = cfg.get("stages", "xfmy")
        if "x" in stg:
            load_x(T, prompt, seq)
        for l in range(DEPTH):
            if "f" in stg:
                ffn(l, 0, T)
            if "m" in stg:
                mixer(l, T, Tc, prompt, seq)
            if "f" in stg:
                ffn(l, 1, T, do_barrier=(l == DEPTH - 1) or ("m" not in stg and False))
        if "y" in stg:
            store_y(T, prompt, seq)
    tr.final_wait()

    sem_ctx = {}
    sems = {}
    for name in tr.sem_names():
        g = nc.semaphore("s_" + name.replace(":", "_"))
        sems[name] = g.__enter__()
        sem_ctx[name] = g
    with nc.Block() as block:
        @block.sync
        def _(e):
            tr.replay("sp", e, sems)

        @block.scalar
        def _(e):
            tr.replay("act", e, sems)

        @block.vector
        def _(e):
            tr.replay("dve", e, sems)

        @block.gpsimd
        def _(e):
            tr.replay("pool", e, sems)

        @block.tensor
        def _(e):
            tr.replay("pe", e, sems)
    for g in sem_ctx.values():
        g.__exit__(None, None, None)
    for g in reversed(ctx):
        g.__exit__(None, None, None)
    ninst = {k: len(v) for k, v in tr.ops.items()}
    return nc, ninst


def prep_shared(inp, DEPTH):
    f32 = np.float32
    w1 = np.empty((DEPTH * 2, NJ, 128, KC * 256), f32)
    w2 = np.empty((DEPTH * 2, KC, 128, NJ * 128), f32)
    for l in range(DEPTH):
        for f, (wi, wo_) in enumerate((("ffn1_w_in", "ffn1_w_out"), ("ffn2_w_in", "ffn2_w_out"))):
            a = np.asarray(inp[wi][l], f32).reshape(KC, 128, 2, NJ, 128)
            w1[l * 2 + f] = a.transpose(3, 1, 0, 2, 4).reshape(NJ, 128, KC * 256)
            b = np.asarray(inp[wo_][l], f32).reshape(NJ, 128, KC, 128)
            w2[l * 2 + f] = b.transpose(2, 1, 0, 3).reshape(KC, 128, NJ * 128)
    cols, tok = _perm_cols()
    wf = np.empty((DEPTH, NEXT_F, 128, KC * 128), f32)
    wt = np.empty((DEPTH, 128, KC * NTOK), f32)
    wo = np.empty((DEPTH, KC, 128, KC * 128), f32)
    for l in range(DEPTH):
        w = np.asarray(inp["w_in"][l], f32)
        a = w[:, cols].reshape(KC, 128, NEXT_F, 128)
        wf[l] = a.transpose(2, 1, 0, 3).reshape(NEXT_F, 128, KC * 128)
        t = w[:, tok].reshape(KC, 128, NTOK)
        wt[l] = t.transpose(1, 0, 2).reshape(128, KC * NTOK)
        o = np.asarray(inp["w_out"][l], f32).reshape(KC, 128, KC, 128)
        wo[l] = o.transpose(2, 1, 0, 3).reshape(KC, 128, KC * 128)
    gl = []
    for l in range(DEPTH):
        for nm in ("norm_ffn1_pre", "norm_ffn1_post", "norm_mix_pre", "norm_mix_post", "norm_ffn2_pre", "norm_ffn2_post"):
            gl.append(np.asarray(inp[nm][l], f32).reshape(KC, 128).T)
    gl.append(np.asarray(inp["final_norm"], f32).reshape(KC, 128).T)
    gains = np.concatenate(gl, axis=1)
    sinks = np.asarray(inp["attn_sinks"], f32).reshape(1, DEPTH * 8)
    t64 = _tables(64, 48)
    t32 = _tables(32, 0)
    shared = {
        "w1": w1, "w2": w2, "wf": wf, "wt": wt, "wo": wo, "gains": np.ascontiguousarray(gains), "sinks": sinks,
        "tabfm": np.stack([t64[0], t32[0]]), "tabtm": np.stack([t64[1], t32[1]]),
        "gt": np.stack([t64[2], t32[2]]), "rot": np.stack([t64[3], t32[3]]),
        "masks": _attn_masks(), "ident": np.eye(128, dtype=f32), "perm": _perm_matrix(),
        "meta": np.asarray(inp["meta_tokens"], f32),
    }
    return shared


_CACHE = {}


def run(inp, cfg, n_cores):
    NPS, S, NSS, DEPTH = cfg["NPS"], cfg["S"], cfg["NSS"], cfg["DEPTH"]
    key = (NPS, S, NSS, DEPTH, cfg.get("stages", "xfmy"))
    if key not in _CACHE:
        _CACHE[key] = build(cfg)
    nc, ninst = _CACHE[key]
    shared = prep_shared(inp, DEPTH)
    f32 = np.float32
    in_maps = []
    for c in range(n_cores):
        m = dict(shared)
        m["xp"] = np.ascontiguousarray(np.asarray(inp["x_prompt"], f32)[c * NPS:(c + 1) * NPS])
        m["xs"] = np.ascontiguousarray(np.asarray(inp["x_sample"], f32)[c * NSS:(c + 1) * NSS]).reshape(NSS * 32, D)
        m["ck"] = np.ascontiguousarray(np.asarray(inp["cache_swa_k"], f32)[:, c * NSS:(c + 1) * NSS]).reshape(DEPTH, NSS, 128, 128)
        m["cv"] = np.ascontiguousarray(np.asarray(inp["cache_swa_v"], f32)[:, c * NSS:(c + 1) * NSS]).reshape(DEPTH, NSS, 128, 128)
        m["st"] = np.ascontiguousarray(np.asarray(inp["state_ret"], f32)[:, c * NSS:(c + 1) * NSS])
        in_maps.append(m)
    res = run_bass_kernel_spmd(nc, in_maps, core_ids=list(range(n_cores)))
    R = res.results
    yp = np.concatenate([r["yp"] for r in R], axis=0)
    ys = np.concatenate([r["ys"].reshape(NSS, 32, D) for r in R], axis=0)
    pk = np.concatenate([r["pk"].reshape(DEPTH, NPS, 128, 2, 64) for r in R], axis=1)
    pv = np.concatenate([r["pv"].reshape(DEPTH, NPS, 128, 2, 64) for r in R], axis=1)
    pr = np.concatenate([r["pr"] for r in R], axis=1)
    sk = np.concatenate([r["sk"].reshape(DEPTH, NSS, 128, 2, 64) for r in R], axis=1)
    sv = np.concatenate([r["sv"].reshape(DEPTH, NSS, 128, 2, 64) for r in R], axis=1)
    sr = np.concatenate([r["sr"] for r in R], axis=1)
    return (yp, ys, pk, pv, pr, sk, sv, sr)


def kernel(**inputs):
    cfg = {"NPS": 4, "S": 2048, "NSS": 4, "DEPTH": 4}
    return run(inputs, cfg, 8)
```
